# Optimizing a Trainium2 kernel written in Bass

```python
import jax
import jax.numpy as jnp
from jax import lax
import numpy as np

D_MODEL = 1024
BATCH = 8
SEQ = 4096
DEPTH = 2

POOL_WINDOWS = (2, 4, 8, 16)
N_POOL_GROUPS = len(POOL_WINDOWS)
POOL_WIDTH = D_MODEL // 2
POOL_GROUP = POOL_WIDTH // N_POOL_GROUPS
RET_HEADS = 4
RET_QK_DIM = 64
RET_V_DIM = 128
RET_WIDTH = RET_HEADS * RET_V_DIM
RET_CHUNK = 128
ROPE_BASE = 10000.0
AB_SPLITS = (POOL_WIDTH,
             POOL_WIDTH + RET_HEADS * RET_QK_DIM,
             POOL_WIDTH + 2 * RET_HEADS * RET_QK_DIM,
             POOL_WIDTH + 2 * RET_HEADS * RET_QK_DIM + RET_WIDTH)
AB_IN = POOL_WIDTH + 2 * RET_HEADS * RET_QK_DIM + 2 * RET_WIDTH
AB_OUT_IN = POOL_WIDTH + RET_WIDTH
HGRN_HEADS = 8
HGRN_EXPAND = 128
HGRN_FDIM = HGRN_HEADS * HGRN_EXPAND
HGRN_V_DIM = D_MODEL // HGRN_HEADS
HGRN_CHUNK = 32
C_SPLITS = (HGRN_FDIM, 2 * HGRN_FDIM, 2 * HGRN_FDIM + D_MODEL)
C_IN = 2 * HGRN_FDIM + 2 * D_MODEL
PEER_HEADS = 8
PEER_NKEYS = 128
PEER_EXPERTS = PEER_NKEYS * PEER_NKEYS
PEER_KEY_DIM = 256
PEER_HALF = PEER_KEY_DIM // 2
PEER_TOPK = 16
PEER_BLOCK = 128
DN_ALPHA = (2 * DEPTH) ** 0.25
DN_BETA = (8 * DEPTH) ** -0.25
LN_EPS = 1e-5
N_EVEN = (DEPTH + 1) // 2
N_ODD = DEPTH // 2

kernel_name = "hybrid_pool_retention_hgrn2_peer_deepnorm"


def layer_norm(x, g, b):
    xf = x.astype(jnp.float32)
    mu = xf.mean(-1, keepdims=True)
    var = jnp.square(xf - mu).mean(-1, keepdims=True)
    return ((xf - mu) * lax.rsqrt(var + LN_EPS) * g + b).astype(x.dtype)


def head_layernorm(o, g):
    B_, S_, H, d = o.shape
    mu = o.mean(-1, keepdims=True)
    var = jnp.square(o - mu).mean(-1, keepdims=True)
    return ((o - mu) * lax.rsqrt(var + LN_EPS)).reshape(B_, S_, H * d) * g


def head_rmsnorm(o, g):
    B_, S_, H, d = o.shape
    y = o * lax.rsqrt(jnp.square(o).mean(-1, keepdims=True) + LN_EPS)
    return y.reshape(B_, S_, H * d) * g


def to_chunks(t, chunk, heads):
    B_, S_ = t.shape[0], t.shape[1]
    t = t.astype(jnp.float32).reshape(B_, S_ // chunk, chunk, heads, -1)
    return t.transpose(1, 0, 3, 2, 4)


def from_chunks(t):
    N, B_, H, C, d = t.shape
    return t.transpose(1, 0, 3, 2, 4).reshape(B_, N * C, H, d)


def pool_mixer(u, pool_w, pool_scale):
    B_, S_, _ = u.shape
    ug = u.astype(jnp.float32).reshape(B_, S_, N_POOL_GROUPS, POOL_GROUP)
    count = jnp.arange(1, S_ + 1, dtype=jnp.float32)
    outs = []
    for gi, w in enumerate(POOL_WINDOWS):
        xg = ug[:, :, gi]
        cs = jnp.cumsum(xg, axis=1)
        lagged = jnp.pad(cs, ((0, 0), (w, 0), (0, 0)))[:, :S_]
        mean = (cs - lagged) / jnp.minimum(count, float(w))[None, :, None]
        outs.append(mean - xg)
    p = jnp.stack(outs, axis=2).astype(u.dtype)
    y = jnp.einsum('bsgc,gcd->bsgd', p, pool_w).reshape(B_, S_, POOL_WIDTH)
    return y * pool_scale


def rotary(x, pos):
    half = x.shape[-1] // 2
    inv = 1.0 / (ROPE_BASE ** jnp.linspace(0.0, 1.0, half, dtype=jnp.float32))
    ang = pos[:, None] * inv[None, :]
    cos = jnp.cos(ang)[None, :, None, :]
    sin = jnp.sin(ang)[None, :, None, :]
    xf = x.astype(jnp.float32)
    x1, x2 = xf[..., :half], xf[..., half:]
    return jnp.concatenate([x1 * cos - x2 * sin, x2 * cos + x1 * sin], axis=-1)


def retention(q, k, v):
    C = RET_CHUNK
    H = q.shape[2]
    log_gamma = jnp.log1p(-jnp.exp2(-5.0 - jnp.arange(H, dtype=jnp.float32)))
    idx = jnp.arange(C, dtype=jnp.float32)
    diff = idx[:, None] - idx[None, :]
    dmask = jnp.where(diff >= 0, jnp.exp(log_gamma[:, None, None] * jnp.maximum(diff, 0.0)), 0.0)
    q_decay = jnp.exp(log_gamma[:, None] * (idx[None, :] + 1.0))[None, :, :, None]
    k_decay = jnp.exp(log_gamma[:, None] * (C - 1.0 - idx[None, :]))[None, :, :, None]
    chunk_decay = jnp.exp(log_gamma * C)[None, :, None, None]
    qc, kc, vc = to_chunks(q, C, H), to_chunks(k, C, H), to_chunks(v, C, H)
    scores = jnp.einsum('nbhcd,nbhsd->nbhcs', qc, kc) * dmask
    intra = jnp.einsum('nbhcs,nbhse->nbhce', scores, vc)

    def step(state, inp):
        q_n, k_n, v_n = inp
        cross = jnp.einsum('bhcd,bhde->bhce', q_n, state) * q_decay
        state = state * chunk_decay + jnp.einsum('bhsd,bhse->bhde', k_n * k_decay, v_n)
        return state, cross

    state0 = jnp.zeros((q.shape[0], H, q.shape[3], v.shape[3]), jnp.float32)
    _, cross = lax.scan(step, state0, (qc, kc, vc))
    return from_chunks(intra + cross)


def hgrn2_chunkwise(q, k, v, logf):
    C, H = HGRN_CHUNK, HGRN_HEADS
    qc, kc, vc, lc = (to_chunks(t, C, H) for t in (q, k, v, logf))
    b = jnp.cumsum(lc, axis=3)
    b_last = b[:, :, :, -1:, :]
    q_in = qc * jnp.exp(b)
    k_in = kc * jnp.exp(-b)
    k_out = kc * jnp.exp(b_last - b)
    causal = jnp.tril(jnp.ones((C, C), dtype=bool))
    scores = jnp.where(causal, jnp.einsum('nbhcd,nbhsd->nbhcs', q_in, k_in), 0.0)
    intra = jnp.einsum('nbhcs,nbhse->nbhce', scores, vc)
    chunk_decay = jnp.exp(b_last[:, :, :, 0, :])

    def step(state, inp):
        q_n, k_n, v_n, d_n = inp
        cross = jnp.einsum('bhcd,bhde->bhce', q_n, state)
        state = state * d_n[..., None] + jnp.einsum('bhsd,bhse->bhde', k_n, v_n)
        return state, cross

    state0 = jnp.zeros((q.shape[0], H, HGRN_EXPAND, HGRN_V_DIM), jnp.float32)
    _, cross = lax.scan(step, state0, (q_in, k_out, vc, chunk_decay))
    return from_chunks(intra + cross)


def mixer_ab(x, w_in, pool_w, pool_scale, ret_norm_g, w_out):
    B_, S_, _ = x.shape
    h = x @ w_in
    u, q, k, v, g = jnp.split(h, AB_SPLITS, axis=-1)
    y_a = pool_mixer(u, pool_w, pool_scale)
    pos = jnp.arange(S_, dtype=jnp.float32)
    q = rotary(q.reshape(B_, S_, RET_HEADS, RET_QK_DIM), pos)
    k = rotary(k.reshape(B_, S_, RET_HEADS, RET_QK_DIM), pos) * (RET_QK_DIM ** -0.5)
    o = retention(q, k, v.reshape(B_, S_, RET_HEADS, RET_V_DIM))
    y_b = head_layernorm(o, ret_norm_g) * jax.nn.silu(g.astype(jnp.float32))
    y = jnp.concatenate([y_a.astype(x.dtype), y_b.astype(x.dtype)], axis=-1)
    return y @ w_out


def mixer_c(x, w_in, lower_bound, norm_g, w_out):
    h = x @ w_in
    q, fz, i, g = jnp.split(h, C_SPLITS, axis=-1)
    f = lower_bound + (1.0 - lower_bound) * jax.nn.sigmoid(fz.astype(jnp.float32))
    o = hgrn2_chunkwise(q, 1.0 - f, i, jnp.log(f))
    y = head_rmsnorm(o, norm_g) * jax.nn.silu(g.astype(jnp.float32))
    return y.astype(x.dtype) @ w_out


def peer(x, w_q, sub_keys, u_tab, v_tab):
    B_, S_, D = x.shape
    xb = x.reshape((B_ * S_) // PEER_BLOCK, PEER_BLOCK, D)
    kk = PEER_TOPK * PEER_TOPK

    def block(xt):
        T = xt.shape[0]
        q = (xt @ w_q).reshape(T, PEER_HEADS, 2, PEER_HALF)
        s = jnp.einsum('thpd,hpkd->thpk', q, sub_keys).astype(jnp.float32)
        s_top, i_top = lax.top_k(s, PEER_TOPK)
        cand = (s_top[:, :, 0, :, None] + s_top[:, :, 1, None, :]).reshape(T, PEER_HEADS, kk)
        cand_id = (i_top[:, :, 0, :, None] * PEER_NKEYS + i_top[:, :, 1, None, :]).reshape(T, PEER_HEADS, kk)
        best, pos = lax.top_k(cand, PEER_TOPK)
        eid = jnp.take_along_axis(cand_id, pos, axis=-1)
        gate = jax.nn.softmax(best, axis=-1)
        u_sel = jnp.take(u_tab, eid, axis=0)
        v_sel = jnp.take(v_tab, eid, axis=0)
        act = jax.nn.gelu(jnp.einsum('td,thkd->thk', xt, u_sel).astype(jnp.float32), approximate=False)
        coef = (gate * act).astype(xt.dtype)
        return jnp.einsum('thk,thkd->td', coef, v_sel)

    return lax.map(block, xb).reshape(B_, S_, D)


def setup_inputs(seed: int = 0) -> dict:
    key = jax.random.key(seed)
    ks = jax.random.split(key, 16)

    def nrm(k, shape, scale):
        return jax.random.normal(k, shape, jnp.float32) * scale

    return {
        "x": nrm(ks[0], (BATCH, SEQ, D_MODEL), 1.0),
        "ab_w_in": nrm(ks[1], (N_EVEN, D_MODEL, AB_IN), D_MODEL ** -0.5),
        "pool_w": nrm(ks[2], (N_EVEN, N_POOL_GROUPS, POOL_GROUP, POOL_GROUP), POOL_GROUP ** -0.5),
        "pool_scale": 1.0 + nrm(ks[3], (N_EVEN, POOL_WIDTH), 0.02),
        "ret_norm_g": 1.0 + nrm(ks[4], (N_EVEN, RET_WIDTH), 0.02),
        "ab_w_out": nrm(ks[5], (N_EVEN, AB_OUT_IN, D_MODEL), AB_OUT_IN ** -0.5 * DN_BETA),
        "c_w_in": nrm(ks[6], (N_ODD, D_MODEL, C_IN), D_MODEL ** -0.5),
        "hgrn_lb": nrm(ks[7], (DEPTH, HGRN_FDIM), 0.1),
        "hgrn_norm_g": 1.0 + nrm(ks[8], (N_ODD, HGRN_HEADS * HGRN_V_DIM), 0.02),
        "c_w_out": nrm(ks[9], (N_ODD, D_MODEL, D_MODEL), D_MODEL ** -0.5 * DN_BETA),
        "peer_w_q": nrm(ks[10], (DEPTH, D_MODEL, PEER_HEADS * PEER_KEY_DIM), D_MODEL ** -0.5),
        "peer_sub_keys": nrm(ks[11], (DEPTH, PEER_HEADS, 2, PEER_NKEYS, PEER_HALF), PEER_HALF ** -0.5),
        "peer_u": nrm(ks[12], (DEPTH, PEER_EXPERTS, D_MODEL), D_MODEL ** -0.5),
        "peer_v": nrm(ks[13], (DEPTH, PEER_EXPERTS, D_MODEL), PEER_HEADS ** -0.5 * DN_BETA),
        "ln_g": 1.0 + nrm(ks[14], (DEPTH, 2, D_MODEL), 0.02),
        "ln_b": nrm(ks[15], (DEPTH, 2, D_MODEL), 0.02),
    }


def reference(x, ab_w_in, pool_w, pool_scale, ret_norm_g, ab_w_out, c_w_in, hgrn_lb,
              hgrn_norm_g, c_w_out, peer_w_q, peer_sub_keys, peer_u, peer_v, ln_g, ln_b):
    lb_p = jax.nn.softmax(hgrn_lb.astype(jnp.float32), axis=0)
    lower_bounds = jnp.cumsum(lb_p, axis=0) - lb_p[0]
    h = x
    for layer in range(DEPTH):
        j = layer // 2
        if layer % 2 == 0:
            mix = mixer_ab(h, ab_w_in[j], pool_w[j], pool_scale[j], ret_norm_g[j], ab_w_out[j])
        else:
            mix = mixer_c(h, c_w_in[j], lower_bounds[layer], hgrn_norm_g[j], c_w_out[j])
        h = layer_norm(DN_ALPHA * h + mix.astype(h.dtype), ln_g[layer, 0], ln_b[layer, 0])
        ffn = peer(h, peer_w_q[layer], peer_sub_keys[layer], peer_u[layer], peer_v[layer])
        h = layer_norm(DN_ALPHA * h + ffn.astype(h.dtype), ln_g[layer, 1], ln_b[layer, 1])
    return h
```

```python
import math
import numpy as np
import ml_dtypes
from contextlib import ExitStack
import concourse.bass as bass
import concourse.mybir as mybir
from concourse.bass_utils import run_bass_kernel_spmd

F32 = mybir.dt.float32
BF16 = mybir.dt.bfloat16
U32 = mybir.dt.uint32
ALU = mybir.AluOpType
AF = mybir.ActivationFunctionType
AX = mybir.AxisListType

ENGS = ("pe", "act", "dve", "pool", "sp")
NDMA_SEM = 6

D = 1024
ALPHA = float((2 * 2) ** 0.25)
EPS = 1e-5
NEG = -1.0e30
import os
DBG = int(os.environ.get('KDBG', '0'))


class Op:
    __slots__ = ("eng", "fn", "reads", "writes", "dma", "deps", "needs_inc", "tok", "waits")

    def __init__(self, eng, fn, reads, writes, dma):
        self.eng = eng
        self.fn = fn
        self.reads = reads
        self.writes = writes
        self.dma = dma
        self.deps = ()
        self.needs_inc = False
        self.tok = None
        self.waits = ()


class Prog:
    def __init__(self, nc):
        self.nc = nc
        self.ops = []
        self.gstack = ExitStack()
        self.stack = ExitStack()
        self._n = 0
        st = self.gstack
        self.esem = {e: st.enter_context(nc.semaphore(f"s_{e}")) for e in ENGS}
        self.dsem = {e: [st.enter_context(nc.semaphore(f"d_{e}{i}")) for i in range(NDMA_SEM)]
                     for e in ("sp", "act", "pool")}
        self.ecnt = {e: 0 for e in ENGS}
        self.dcnt = {e: [0] * NDMA_SEM for e in self.dsem}
        self.drr = {e: 0 for e in self.dsem}
        self.waited = {e: {} for e in ENGS}
        self.counts = {e: 0 for e in ENGS}

    def sb(self, shape, dtype, name=None):
        self._n += 1
        return self.stack.enter_context(
            self.nc.sbuf_tensor(name or f"sb{self._n}", list(shape), dtype))

    def ps(self, shape, dtype, name=None):
        self._n += 1
        return self.stack.enter_context(
            self.nc.psum_tensor(name or f"ps{self._n}", list(shape), dtype))

    def add(self, eng, fn, reads=(), writes=(), dma=False):
        o = Op(eng, fn, tuple(reads), tuple(writes), dma)
        self.ops.append(o)
        return o

    def pe(self, fn, r=(), w=()):
        return self.add("pe", fn, r, w)

    def act(self, fn, r=(), w=()):
        return self.add("act", fn, r, w)

    def dve(self, fn, r=(), w=()):
        return self.add("dve", fn, r, w)

    def pool(self, fn, r=(), w=()):
        return self.add("pool", fn, r, w)

    def dma(self, q, out, in_, r=(), w=(), **kw):
        return self.add(q, lambda e: e.dma_start(out=out, in_=in_, **kw), r, w, dma=True)

    def barrier(self):
        for e in ENGS:
            o = self.add(e, None, (), ())
            o.dma = "barrier"

    def finalize(self):
        self.flush()
        self.gstack.close()
        return self.counts

    def flush(self):
        self.barrier()
        nc = self.nc
        ops = self.ops
        last_w = {}
        readers = {}
        dependents = [False] * len(ops)
        last_real = {}
        for i, o in enumerate(ops):
            if o.dma == "barrier":
                for j in last_real.values():
                    dependents[j] = True
                last_w.clear()
                readers.clear()
                continue
            deps = set()
            for k in o.reads:
                j = last_w.get(k)
                if j is not None:
                    deps.add(j)
            for k in o.writes:
                j = last_w.get(k)
                if j is not None:
                    deps.add(j)
                rd = readers.get(k)
                if rd:
                    deps.update(rd[0].values())
                    deps.update(rd[1])
            deps.discard(i)
            if o.eng == "pe" and not o.dma:
                deps = {j for j in deps if not (ops[j].eng == "pe" and not ops[j].dma)}
            o.deps = sorted(deps)
            for j in o.deps:
                dependents[j] = True
            for k in o.writes:
                last_w[k] = i
                readers[k] = None
            for k in o.reads:
                if k not in o.writes:
                    rd = readers.get(k)
                    if rd is None:
                        rd = readers[k] = ({}, [])
                    if o.dma:
                        rd[1].append(i)
                    else:
                        rd[0][o.eng] = i
            if not o.dma and o.fn is not None:
                last_real[o.eng] = i
        esem, dsem, ecnt, dcnt, drr, waited = (self.esem, self.dsem, self.ecnt, self.dcnt,
                                               self.drr, self.waited)
        for i, o in enumerate(ops):
            ws = []
            w = waited[o.eng]

            def need(tok):
                if tok is None:
                    return
                sem, val = tok
                key = id(sem)
                if w.get(key, 0) >= val:
                    return
                w[key] = val
                ws.append((sem, val))

            if o.dma == "barrier":
                for e2 in ENGS:
                    if ecnt[e2] > 0:
                        need((esem[e2], ecnt[e2]))
                for q in dsem:
                    for s in range(NDMA_SEM):
                        if dcnt[q][s] > 0:
                            need((dsem[q][s], dcnt[q][s]))
                o.waits = ws
                continue
            for j in o.deps:
                need(ops[j].tok)
            if o.dma:
                s = drr[o.eng] % NDMA_SEM
                drr[o.eng] += 1
                prev = dcnt[o.eng][s]
                if prev > 0:
                    need((dsem[o.eng][s], prev))
                dcnt[o.eng][s] = prev + 16
                o.tok = (dsem[o.eng][s], prev + 16)
                o.needs_inc = True
            elif o.fn is not None and dependents[i]:
                ecnt[o.eng] += 1
                o.tok = (esem[o.eng], ecnt[o.eng])
                o.needs_inc = True
            o.waits = ws
        by_eng = {e: [o for o in ops if o.eng == e] for e in ENGS}

        def emit(eng_obj, lst):
            for o in lst:
                for sem, val in o.waits:
                    eng_obj.wait_ge(sem, val)
                if o.fn is None:
                    continue
                ins = o.fn(eng_obj)
                if o.needs_inc:
                    ins.then_inc(o.tok[0], 16 if o.dma else 1)

        with nc.Block() as block:
            @block.sync
            def _(e):
                emit(e, by_eng["sp"])

            @block.scalar
            def _(e):
                emit(e, by_eng["act"])

            @block.vector
            def _(e):
                emit(e, by_eng["dve"])

            @block.gpsimd
            def _(e):
                emit(e, by_eng["pool"])

            @block.tensor
            def _(e):
                emit(e, by_eng["pe"])
        self.stack.close()
        self.stack = ExitStack()
        self.ops = []
        for e in ENGS:
            self.counts[e] += len(by_eng[e])


class Stage:
    def __init__(self, p, dram, banks=True):
        self.p = p
        self.dram = dram
        if banks:
            self.PS = [p.ps([128, 512], F32, name=f"bank{p._n}_{i}") for i in range(8)]
        self.rr = {}
        self.ident = p.sb([128, 128], BF16)
        p.dma("sp", self.ident[:], dram["ident"], w=["ident"])
        self.epsb = p.sb([128, 1], F32)
        p.dve(lambda e: e.memset(self.epsb[:], EPS), w=["epsb"])
        self.lng = p.sb([128, 1024], F32)
        self.lnb = p.sb([128, 1024], F32)

    def load_ln(self, li):
        self.p.dma("sp", self.lng[:], self.dram["lng"][li], w=["lng"])
        self.p.dma("sp", self.lnb[:], self.dram["lnb"][li], w=["lnb"])

    def bank(self, pool):
        pk = tuple(pool)
        i = self.rr.get(pk, 0)
        self.rr[pk] = i + 1
        b = pool[i % len(pool)]
        return self.PS[b], ("ps", b)


def capture_ops(p, fn, *a):
    saved = p.ops
    p.ops = []
    fn(*a)
    got = p.ops
    p.ops = saved
    return got


def mm(p, out, lhsT, rhs, start, stop, r, w):
    p.pe(lambda e: e.matmul(out, lhsT=lhsT, rhs=rhs, start=start, stop=stop), r=r, w=w)


def load_transpose(st, src, ntile, xt, xb, hT, tag, banks):
    p = st.p
    nb = xb.shape[1]
    for i in range(ntile):
        ib = i % nb
        p.dma("sp", xt[:, i, :], src[i * 128:(i + 1) * 128, :], w=[(tag, "xt", i)])
        p.act(lambda e, i=i, ib=ib: e.activation(out=xb[:, ib, :], in_=xt[:, i, :], func=AF.Copy),
              r=[(tag, "xt", i)], w=[(tag, "xb", ib)])
        bk, bkey = st.bank(banks)
        bv = bk[:].bitcast(BF16).rearrange("p (k n) -> p k n", n=128)
        for k in range(8):
            p.pe(lambda e, ib=ib, k=k, bv=bv: e.transpose(out=bv[:, k, :], in_=xb[:, ib, k * 128:(k + 1) * 128],
                                                          identity=st.ident[:]),
                 r=[(tag, "xb", ib), "ident"], w=[bkey])
        p.dve(lambda e, i=i, bv=bv: e.tensor_copy(out=hT[:, :, i * 128:(i + 1) * 128], in_=bv),
              r=[bkey], w=[(tag, "hT", i)])


def ln_epilogue(st, mix, mixkeys, xt_ap, xkeys, dst, tag):
    p = st.p
    z = st.z
    for hf in range(2):
        p.dve(lambda e, hf=hf: e.scalar_tensor_tensor(out=z[:, hf * 512:(hf + 1) * 512],
                                                      in0=xt_ap[:, hf * 512:(hf + 1) * 512], scalar=ALPHA,
                                                      in1=mix[hf], op0=ALU.mult, op1=ALU.add),
              r=[mixkeys[hf]] + list(xkeys), w=[("z", hf)])
        p.dve(lambda e, hf=hf: e.bn_stats(out=st.stats[:, hf, :], in_=z[:, hf * 512:(hf + 1) * 512]),
              r=[("z", hf)], w=[("stats", hf)])
    p.dve(lambda e: e.bn_aggr(out=st.mv[:], in_=st.stats[:].rearrange("p a b -> p (a b)")),
          r=[("stats", 0), ("stats", 1)], w=["mv"])
    p.act(lambda e: e.activation(out=st.rstd[:], in_=st.mv[:, 1:2], func=AF.Sqrt, bias=st.epsb[:], scale=1.0),
          r=["mv", "epsb"], w=["rstd0"])
    p.dve(lambda e: e.reciprocal(out=st.rstd[:], in_=st.rstd[:]), r=["rstd0"], w=["rstd0", "rstd"])
    p.dve(lambda e: e.tensor_scalar(out=z[:], in0=z[:], scalar1=st.mv[:, 0:1], scalar2=st.rstd[:],
                                    op0=ALU.subtract, op1=ALU.mult),
          r=["mv", "rstd", ("z", 0), ("z", 1)], w=["zn"])
    p.pool(lambda e: e.tensor_tensor(out=z[:], in0=z[:], in1=st.lng[:], op=ALU.mult), r=["zn", "lng"], w=["zn2"])
    p.pool(lambda e: e.tensor_tensor(out=z[:], in0=z[:], in1=st.lnb[:], op=ALU.add), r=["zn2", "lnb"], w=["zn3"])
    p.dma("sp", dst, z[:], r=["zn3"], w=[(tag, "dst"), ("z", 0), ("z", 1), "zn", "zn2", "zn3"])


def alloc_ln(st):
    p = st.p
    st.z = p.sb([128, 1024], F32)
    st.stats = p.sb([128, 2, 6], F32)
    st.mv = p.sb([128, 2], F32)
    st.rstd = p.sb([128, 1], F32)


def wload(p, dst, src, key, q="pool"):
    p.dma(q, dst[:], src.rearrange("(k p) n -> p k n", p=128), w=[key])


GAMMA = [1.0 - 2.0 ** (-5.0 - h) for h in range(4)]


def stage_ab(p, dram, src, dst, S):
    st = Stage(p, dram)
    alloc_ln(st)
    st.load_ln(0)
    nblk = S // 512
    R4 = [0, 1, 2, 3]
    wfm = p.sb([128, 8, 1536], BF16)
    wtm = p.sb([128, 8, 1024], BF16)
    wout = p.sb([128, 8, 1024], BF16)
    wload(p, wfm, dram["ab_wfm"], "wfm")
    wload(p, wtm, dram["ab_wtm"], "wtm")
    wload(p, wout, dram["ab_wout"], "wout")
    dq = p.sb([64, 4, 128], F32)
    dk = p.sb([64, 4, 128], F32)
    maskT = p.sb([128, 128], F32)
    pm = p.sb([128, 4, 3, 128], BF16)
    poolw = p.sb([128, 4, 128], BF16)
    pscale = p.sb([128, 4], F32)
    retg = p.sb([128, 4], F32)
    onesd = p.sb([128, 128], F32)
    p.dma("sp", dq[:], dram["ab_dq"], w=["dq"])
    p.dma("sp", dk[:], dram["ab_dk"], w=["dk"])
    p.dma("sp", maskT[:], dram["ab_maskT"], w=["maskT"])
    p.dma("sp", pm[:], dram["ab_pm"], w=["pm"])
    p.dma("pool", poolw[:], dram["ab_poolw"], w=["poolw"])
    p.dma("sp", pscale[:], dram["ab_pscale"], w=["pscale"])
    p.dma("sp", retg[:], dram["ab_retg"], w=["retg"])
    p.dve(lambda e: e.memset(onesd[:], 1.0 / 128.0), w=["onesd"])

    xt = p.sb([128, 4, 1024], F32)
    xb = p.sb([128, 4, 1024], BF16)
    hT = p.sb([128, 8, 512], BF16)
    cosb = p.sb([64, 512], F32)
    sinb = p.sb([64, 512], F32)
    qT = [p.sb([64, 512], BF16) for _ in range(4)]
    kT = [p.sb([64, 512], BF16) for _ in range(4)]
    sgT = p.sb([128, 4, 512], F32)
    u_tm = [p.sb([128, 4, 512], BF16) for _ in range(2)]
    v_tm = p.sb([128, 4, 512], BF16)
    ktm = p.sb([128, 4, 4, 64], BF16)
    t1 = p.sb([64, 512], F32)
    t2 = p.sb([64, 512], F32)
    scTm = [p.sb([128, 128], BF16) for _ in range(4)]
    Tst = [p.sb([64, 128], F32) for _ in range(4)]
    stbf = [p.sb([64, 128], BF16) for _ in range(4)]
    o_sb = p.sb([128, 512], F32)
    cen = p.sb([128, 512], F32)
    sq = p.sb([128, 512], F32)
    sd = p.sb([128, 512], F32)
    yT = [p.sb([128, 512], BF16) for _ in range(8)]
    pT_sb = p.sb([128, 512], BF16)
    for h in range(4):
        p.dve(lambda e, h=h: e.memset(Tst[h][:], 0.0), w=[("T", h)])
        p.dve(lambda e, h=h: e.memset(stbf[h][:], 0.0), w=[("stbf", h)])

    if DBG == 99:
        print("SBUF free stage_ab", p.nc.sbuf_bytes_remaining)
    hTk = [("ab", "hT", i) for i in range(4)]
    for B in range(nblk):
        ub = B % 2
        load_transpose(st, src[B * 512:(B + 1) * 512, :], 4, xt, xb, hT, "ab", R4)
        p.dma("sp", cosb[:], dram["ab_cos"][:, B * 512:(B + 1) * 512], w=["cosb"])
        p.dma("sp", sinb[:], dram["ab_sin"][:, B * 512:(B + 1) * 512], w=["sinb"])
        for h in range(4):
            for which in range(2):
                col0 = h * 256 + which * 128
                dst_t = (qT if which == 0 else kT)[h]
                dkey = ("qT" if which == 0 else "kT", h)
                dec = dq if which == 0 else dk
                b1, k1 = st.bank(R4)
                b2, k2 = st.bank(R4)
                for k in range(8):
                    mm(p, b1[0:64, :], wfm[:, k, col0:col0 + 64], hT[:, k, :], k == 0, k == 7,
                       ["wfm"] + hTk, [k1])
                for k in range(8):
                    mm(p, b2[0:64, :], wfm[:, k, col0 + 64:col0 + 128], hT[:, k, :], k == 0, k == 7,
                       ["wfm"] + hTk, [k2])
                p.dve(lambda e, b1=b1: e.tensor_tensor(out=t1[:], in0=b1[0:64, :], in1=cosb[:], op=ALU.mult),
                      r=[k1, "cosb"], w=["t1"])
                p.dve(lambda e, b2=b2: e.tensor_tensor(out=t2[:], in0=b2[0:64, :], in1=sinb[:], op=ALU.mult),
                      r=[k2, "sinb"], w=["t2"])
                p.pool(lambda e: e.tensor_tensor(out=t1[:], in0=t1[:], in1=t2[:], op=ALU.add),
                       r=["t1", "t2"], w=["t1"])
                p.pool(lambda e, dst_t=dst_t, dec=dec, h=h: e.tensor_tensor(
                    out=dst_t[:].rearrange("p (c j) -> p c j", j=128),
                    in0=t1[:].rearrange("p (c j) -> p c j", j=128),
                    in1=dec[:, h, :].unsqueeze(1).broadcast_to([64, 4, 128]), op=ALU.mult),
                    r=["t1", "dq", "dk"], w=[dkey])
            bg, kg = st.bank(R4)
            for k in range(8):
                mm(p, bg[:], wfm[:, k, 1024 + h * 128:1024 + (h + 1) * 128], hT[:, k, :], k == 0, k == 7,
                   ["wfm"] + hTk, [kg])
            p.act(lambda e, bg=bg, h=h: e.activation(out=sgT[:, h, :], in_=bg[:], func=AF.Silu),
                  r=[kg], w=[("sgT", h)])
        for i in range(4):
            for grp in range(2):
                bk, kk = st.bank(R4)
                for k in range(8):
                    mm(p, bk[:], hT[:, k, i * 128:(i + 1) * 128], wtm[:, k, grp * 512:(grp + 1) * 512],
                       k == 0, k == 7, ["wtm"] + hTk, [kk])
                if grp == 0:
                    p.act(lambda e, bk=bk, i=i, ub=ub: e.activation(out=u_tm[ub][:, i, :], in_=bk[:], func=AF.Copy),
                          r=[kk], w=[("u_tm", ub, i)])
                else:
                    p.act(lambda e, bk=bk, i=i: e.activation(out=v_tm[:, i, :], in_=bk[:], func=AF.Copy),
                          r=[kk], w=[("v_tm", i)])
        bk, kk = st.bank(R4)
        bv = bk[:].bitcast(BF16).rearrange("p (h i d) -> p h i d", h=4, i=4)
        for h in range(4):
            for i in range(4):
                p.pe(lambda e, h=h, i=i, bv=bv: e.transpose(out=bv[:, h, i, :], in_=kT[h][:, i * 128:(i + 1) * 128],
                                                            identity=st.ident[0:64, 0:64]),
                     r=[("kT", h), "ident"], w=[kk])
        p.act(lambda e, bv=bv: e.activation(out=ktm[:], in_=bv, func=AF.Copy), r=[kk], w=["ktm"])
        for i in range(4):
            kvb = []
            for h in range(4):
                sc, ks = st.bank(R4)
                mm(p, sc[:, 0:128], kT[h][:, i * 128:(i + 1) * 128], qT[h][:, i * 128:(i + 1) * 128], True, True,
                   [("kT", h), ("qT", h)], [ks])
                p.dve(lambda e, sc=sc, h=h: e.tensor_tensor(out=scTm[h][:], in0=sc[:, 0:128], in1=maskT[:],
                                                            op=ALU.mult),
                      r=[ks, "maskT"], w=[("scTm", h)])
            kvl = []
            for h in range(4):
                kv, kk2 = st.bank(R4)
                mm(p, kv[0:64, 0:128], ktm[:, h, i, :], v_tm[:, i, h * 128:(h + 1) * 128], True, True,
                   ["ktm", ("v_tm", i)], [kk2])
                kvl.append((kv, kk2))
            for h in range(4):
                ob, ok = st.PS[4 + h], ("ps", 4 + h)
                mm(p, ob[:, i * 128:(i + 1) * 128], v_tm[:, i, h * 128:(h + 1) * 128], scTm[h][:], True, False,
                   [("v_tm", i), ("scTm", h)], [ok])
                mm(p, ob[:, i * 128:(i + 1) * 128], stbf[h][:], qT[h][:, i * 128:(i + 1) * 128], False, True,
                   [("stbf", h), ("qT", h)], [ok])
            for h in range(4):
                kv, kk2 = kvl[h]
                gC = float(GAMMA[h] ** 128)
                p.dve(lambda e, kv=kv, h=h, gC=gC: e.scalar_tensor_tensor(
                    out=Tst[h][:], in0=Tst[h][:], scalar=gC, in1=kv[0:64, 0:128], op0=ALU.mult, op1=ALU.add),
                    r=[kk2, ("T", h)], w=[("T", h)])
                p.act(lambda e, h=h, gC=gC: e.activation(out=stbf[h][:], in_=Tst[h][:], func=AF.Copy, scale=gC),
                      r=[("T", h)], w=[("stbf", h)])
        for h in range(4):
            ob, ok = st.PS[4 + h], ("ps", 4 + h)
            p.act(lambda e, ob=ob: e.activation(out=o_sb[:], in_=ob[:], func=AF.Copy), r=[ok], w=["o_sb"])
            mb, mk = st.bank(R4)
            mm(p, mb[:], onesd[:], o_sb[:], True, True, ["onesd", "o_sb"], [mk])
            p.dve(lambda e, mb=mb: e.tensor_tensor(out=cen[:], in0=o_sb[:], in1=mb[:], op=ALU.subtract),
                  r=["o_sb", mk], w=["cen"])
            p.pool(lambda e: e.tensor_tensor(out=sq[:], in0=cen[:], in1=cen[:], op=ALU.mult), r=["cen"], w=["sq"])
            vb, vk = st.bank(R4)
            mm(p, vb[:], onesd[:], sq[:], True, True, ["onesd", "sq"], [vk])
            p.act(lambda e, vb=vb: e.activation(out=sd[:], in_=vb[:], func=AF.Sqrt, bias=st.epsb[:], scale=1.0),
                  r=[vk, "epsb"], w=["sd"])
            p.dve(lambda e: e.reciprocal(out=sd[:], in_=sd[:]), r=["sd"], w=["sd"])
            p.pool(lambda e: e.tensor_tensor(out=cen[:], in0=cen[:], in1=sd[:], op=ALU.mult),
                   r=["cen", "sd"], w=["cen"])
            p.dve(lambda e, h=h: e.scalar_tensor_tensor(out=yT[4 + h][:], in0=cen[:], scalar=retg[:, h:h + 1],
                                                        in1=sgT[:, h, :], op0=ALU.mult, op1=ALU.mult),
                  r=["cen", "retg", ("sgT", h)], w=[("yT", 4 + h)])
        for g in range(4):
            pp, pk = st.bank(R4)
            for i in range(4):
                n = B * 4 + i
                ucur = u_tm[ub][:, i, g * 128:(g + 1) * 128]
                if n == 0:
                    mm(p, pp[:, 0:128], ucur, pm[:, g, 0, :], True, True, [("u_tm", ub, i), "pm"], [pk])
                else:
                    if i > 0:
                        uprev, pkey = u_tm[ub][:, i - 1, g * 128:(g + 1) * 128], ("u_tm", ub, i - 1)
                    else:
                        uprev, pkey = u_tm[1 - ub][:, 3, g * 128:(g + 1) * 128], ("u_tm", 1 - ub, 3)
                    mm(p, pp[:, i * 128:(i + 1) * 128], ucur, pm[:, g, 1, :], True, False,
                       [("u_tm", ub, i), "pm"], [pk])
                    mm(p, pp[:, i * 128:(i + 1) * 128], uprev, pm[:, g, 2, :], False, True, [pkey, "pm"], [pk])
            p.act(lambda e, pp=pp: e.activation(out=pT_sb[:], in_=pp[:], func=AF.Copy), r=[pk], w=["pT_sb"])
            ya, yk = st.bank(R4)
            mm(p, ya[:], poolw[:, g, :], pT_sb[:], True, True, ["poolw", "pT_sb"], [yk])
            p.dve(lambda e, ya=ya, g=g: e.tensor_scalar(out=yT[g][:], in0=ya[:], scalar1=pscale[:, g:g + 1],
                                                        scalar2=None, op0=ALU.mult),
                  r=[yk, "pscale"], w=[("yT", g)])
        for i in range(4):
            mix = []
            mkeys = []
            for hf in range(2):
                mb, mk = st.bank(R4)
                for f in range(8):
                    mm(p, mb[:], yT[f][:, i * 128:(i + 1) * 128], wout[:, f, hf * 512:(hf + 1) * 512],
                       f == 0, f == 7, [("yT", f), "wout"], [mk])
                mix.append(mb[:])
                mkeys.append(mk)
            ln_epilogue(st, mix, mkeys, xt[:, i, :], [("ab", "xt", i)],
                        dst[B * 512 + i * 128:B * 512 + (i + 1) * 128, :], ("ab", B, i))
    p.flush()


def stage_c(p, dram, src, dst, S):
    st = Stage(p, dram)
    alloc_ln(st)
    st.load_ln(2)
    nblk = S // 512
    R4 = [0, 1, 2, 3]
    RS = [4, 5]
    RK = [6, 7]
    wfm = p.sb([128, 8, 3072], BF16)
    wtm = p.sb([128, 8, 1024], BF16)
    wout = p.sb([128, 8, 1024], BF16)
    wload(p, wfm, dram["c_wfm"], "wfm")
    wload(p, wtm, dram["c_wtm"], "wtm")
    wload(p, wout, dram["c_wout"], "wout")
    lbraw = p.sb([128, 2, 8], F32)
    lb = p.sb([128, 8], F32)
    oml = p.sb([128, 8], F32)
    cg = p.sb([128, 8], F32)
    maskC = p.sb([128, 128], F32)
    rmask = p.sb([128, 512], F32)
    onesd = p.sb([128, 128], F32)
    p.dma("sp", lbraw[:], dram["c_lb"], w=["lbraw"])
    p.dma("sp", cg[:], dram["c_g"], w=["cg"])
    p.dma("sp", maskC[:], dram["c_maskC"], w=["maskC"])
    p.dve(lambda e: e.memset(onesd[:], 1.0 / 128.0), w=["onesd"])
    p.dve(lambda e: e.memset(rmask[:], 1.0), w=["rmask"])
    p.dve(lambda e: e.memset(rmask[:].rearrange("p (c j) -> p c j", j=32)[:, :, 0:1], 0.0), w=["rmask"])
    p.dve(lambda e: e.tensor_tensor(out=lb[:], in0=lbraw[:, 1, :], in1=lbraw[:, 0, :], op=ALU.subtract),
          r=["lbraw"], w=["lb"])
    p.act(lambda e: e.activation(out=lb[:], in_=lb[:], func=AF.Sigmoid), r=["lb"], w=["lb"])
    p.dve(lambda e: e.tensor_scalar(out=oml[:], in0=lb[:], scalar1=-1.0, scalar2=1.0, op0=ALU.mult, op1=ALU.add),
          r=["lb"], w=["oml"])

    xt = p.sb([128, 4, 1024], F32)
    xb = p.sb([128, 4, 1024], BF16)
    hT = p.sb([128, 8, 512], BF16)
    v_tm = p.sb([128, 4, 1024], BF16)
    TS = [[p.sb([128, 512], F32) for _ in range(6)] + [p.sb([128, 512], BF16)] for _ in range(2)]
    dcy = [p.sb([128, 16], F32) for _ in range(4)]
    q_in = [p.sb([128, 512], BF16) for _ in range(4)]
    k_in = [p.sb([128, 512], BF16) for _ in range(4)]
    kotm = [p.sb([128, 4, 128], BF16) for _ in range(4)]
    kotm3 = [p.sb([128, 4, 128], BF16) for _ in range(4)]
    m96 = p.sb([128, 1], F32)
    p.dma("sp", m96[:], dram["c_m96"], w=["m96"])
    sgT = [p.sb([128, 512], F32) for _ in range(4)]
    scTm = [p.sb([128, 128], BF16) for _ in range(4)]
    kvrr = [0]
    state = [p.sb([128, 128], F32) for _ in range(8)]
    stbf = [p.sb([128, 128], BF16) for _ in range(8)]
    o_sb = p.sb([128, 512], F32)
    sq = p.sb([128, 512], F32)
    sd = p.sb([128, 512], F32)
    yT = [p.sb([128, 512], BF16) for _ in range(8)]
    for h in range(8):
        p.dve(lambda e, h=h: e.memset(state[h][:], 0.0), w=[("state", h)])
        p.dve(lambda e, h=h: e.memset(stbf[h][:], 0.0), w=[("stbf", h)])

    if DBG == 99:
        print("SBUF free stage_c", p.nc.sbuf_bytes_remaining)
    hTk = [("c", "hT", i) for i in range(4)]
    for B in range(nblk):
        load_transpose(st, src[B * 512:(B + 1) * 512, :], 4, xt, xb, hT, "c", R4)
        for i in range(4):
            for hf in range(2):
                bk, kk = st.bank(R4)
                for k in range(8):
                    mm(p, bk[:], hT[:, k, i * 128:(i + 1) * 128], wtm[:, k, hf * 512:(hf + 1) * 512],
                       k == 0, k == 7, ["wtm"] + hTk, [kk])
                p.act(lambda e, bk=bk, i=i, hf=hf: e.activation(out=v_tm[:, i, hf * 512:(hf + 1) * 512], in_=bk[:],
                                                                func=AF.Copy), r=[kk], w=[("v_tm", i, hf)])
        if DBG == 1:
            continue
        for hp in range(2):
            def prep_head(hl):
                h = hp * 4 + hl
                s_ = hl % 2
                f_sb, lf, bcs, tmp, kk_sb, eb, k_out = TS[s_]
                RB = [0, 1] if s_ == 0 else [2, 3]
                bf_, kf = st.bank(RB)
                for k in range(8):
                    mm(p, bf_[:], wfm[:, k, 1024 + h * 128:1024 + (h + 1) * 128], hT[:, k, :], k == 0, k == 7,
                       ["wfm"] + hTk, [kf])
                p.act(lambda e, bf_=bf_: e.activation(out=f_sb[:], in_=bf_[:], func=AF.Sigmoid), r=[kf], w=[("f_sb", s_)])
                p.dve(lambda e, h=h: e.tensor_scalar(out=f_sb[:], in0=f_sb[:], scalar1=oml[:, h:h + 1],
                                                     scalar2=lb[:, h:h + 1], op0=ALU.mult, op1=ALU.add),
                      r=[("f_sb", s_), "oml", "lb"], w=[("f_sb", s_)])
                p.act(lambda e: e.activation(out=lf[:], in_=f_sb[:], func=AF.Ln), r=[("f_sb", s_)], w=[("lf", s_)])
                p.pool(lambda e: e.tensor_scalar(out=kk_sb[:], in0=f_sb[:], scalar1=-1.0, scalar2=1.0,
                                                 op0=ALU.mult, op1=ALU.add), r=[("f_sb", s_)], w=[("kk_sb", s_)])
                p.dve(lambda e: e.tensor_tensor_scan(out=bcs[:], data0=rmask[:], data1=lf[:], initial=0.0,
                                                     op0=ALU.mult, op1=ALU.add), r=["rmask", ("lf", s_)], w=[("bcs", s_)])
                p.act(lambda e: e.activation(out=eb[:], in_=bcs[:], func=AF.Exp), r=[("bcs", s_)], w=[("eb", s_)])
                p.dve(lambda e, hl=hl: e.tensor_copy(out=dcy[hl][:],
                                                     in_=eb[:].rearrange("p (c j) -> p c j", j=32)[:, :, 31]),
                      r=[("eb", s_)], w=[("dcy", hl)])
                p.act(lambda e: e.activation(out=tmp[:], in_=bcs[:], func=AF.Exp, scale=-1.0), r=[("bcs", s_)], w=[("tmp", s_)])
                p.pool(lambda e, hl=hl: e.tensor_tensor(out=k_in[hl][:], in0=kk_sb[:], in1=tmp[:], op=ALU.mult),
                       r=[("kk_sb", s_), ("tmp", s_)], w=[("k_in", hl)])
                b3 = bcs[:].rearrange("p (c j) -> p c j", j=32)
                p.pool(lambda e, b3=b3: e.tensor_tensor(out=tmp[:].rearrange("p (c j) -> p c j", j=32),
                                                        in0=b3[:, :, 31:32].broadcast_to([128, 16, 32]), in1=b3,
                                                        op=ALU.subtract), r=[("bcs", s_), ("k_in", hl)], w=[("tmp", s_)])
                p.act(lambda e: e.activation(out=tmp[:], in_=tmp[:], func=AF.Exp), r=[("tmp", s_)], w=[("tmp", s_)])
                p.pool(lambda e: e.tensor_tensor(out=k_out[:], in0=kk_sb[:], in1=tmp[:], op=ALU.mult),
                       r=[("kk_sb", s_), ("tmp", s_)], w=[("k_out", s_)])
                bq, kq = st.bank(RB)
                for k in range(8):
                    mm(p, bq[:], wfm[:, k, h * 128:(h + 1) * 128], hT[:, k, :], k == 0, k == 7, ["wfm"] + hTk, [kq])
                p.dve(lambda e, bq=bq, hl=hl: e.tensor_tensor(out=q_in[hl][:], in0=bq[:], in1=eb[:], op=ALU.mult),
                      r=[kq, ("eb", s_)], w=[("q_in", hl)])
                bg, kg = st.bank(RB)
                for k in range(8):
                    mm(p, bg[:], wfm[:, k, 2048 + h * 128:2048 + (h + 1) * 128], hT[:, k, :], k == 0, k == 7,
                       ["wfm"] + hTk, [kg])
                p.act(lambda e, bg=bg, hl=hl: e.activation(out=sgT[hl][:], in_=bg[:], func=AF.Silu),
                      r=[kg], w=[("sgT", hl)])
                bt, kt = st.bank(RB)
                btv = bt[:].bitcast(BF16)[:, 0:512].rearrange("p (i s) -> p i s", s=128)
                for i in range(4):
                    p.pe(lambda e, i=i, btv=btv: e.transpose(out=btv[:, i, :], in_=k_out[:, i * 128:(i + 1) * 128],
                                                             identity=st.ident[:]), r=[("k_out", s_), "ident"], w=[kt])
                p.act(lambda e, btv=btv, hl=hl: e.activation(out=kotm[hl][:], in_=btv, func=AF.Copy),
                      r=[kt], w=[("kotm", hl)])
                p.dve(lambda e, hl=hl: e.tensor_scalar(out=kotm3[hl][:], in0=kotm[hl][:], scalar1=m96[:, 0:1],
                                                       scalar2=None, op0=ALU.mult),
                      r=[("kotm", hl), "m96"], w=[("kotm3", hl)])
            for pair in range(2):
                ops_a = capture_ops(p, prep_head, 2 * pair)
                ops_b = capture_ops(p, prep_head, 2 * pair + 1)
                kz = 0
                while kz < max(len(ops_a), len(ops_b)):
                    p.ops.extend(ops_a[kz:kz + 2])
                    p.ops.extend(ops_b[kz:kz + 2])
                    kz += 2
            if DBG in (2, 21, 22, 23, 24):
                continue
            for i in range(4):
                sms = []
                for hl in range(4):
                    h = hp * 4 + hl
                    sc, ks = st.bank(RS)
                    sm = scTm[hl]
                    smk = ("scTm", hl)
                    mm(p, sc[:, 0:128], k_in[hl][:, i * 128:(i + 1) * 128], q_in[hl][:, i * 128:(i + 1) * 128],
                       True, True, [("k_in", hl), ("q_in", hl)], [ks])
                    p.dve(lambda e, sc=sc, sm=sm: e.tensor_tensor(out=sm[:], in0=sc[:, 0:128], in1=maskC[:],
                                                                  op=ALU.mult), r=[ks, "maskC"], w=[smk])
                for c4 in range(4):
                    c0 = i * 128 + c4 * 32
                    cc = i * 4 + c4
                    kvs = []
                    for hl in range(4):
                        h = hp * 4 + hl
                        vkey = ("v_tm", i, h // 4)
                        slot = kvrr[0] % 8
                        kvrr[0] += 1
                        kvb = st.PS[RK[slot // 4]]
                        kv = kvb[:, (slot % 4) * 128:(slot % 4 + 1) * 128]
                        kk2 = ("ps", RK[slot // 4])
                        if c4 < 3:
                            mm(p, kv, kotm[hl][c4 * 32:(c4 + 1) * 32, i, :],
                               v_tm[c4 * 32:(c4 + 1) * 32, i, h * 128:(h + 1) * 128], True, True,
                               [("kotm", hl), vkey], [kk2])
                        else:
                            mm(p, kv, kotm3[hl][64:128, i, :],
                               v_tm[64:128, i, h * 128:(h + 1) * 128], True, True,
                               [("kotm3", hl), vkey], [kk2])
                        kvs.append((kv, kk2))
                    for hl in range(4):
                        h = hp * 4 + hl
                        ob, ok = st.PS[hl], ("ps", hl)
                        vsl = v_tm[:, i, h * 128:(h + 1) * 128]
                        vkey = ("v_tm", i, h // 4)
                        sm = scTm[hl]
                        smk = ("scTm", hl)
                        mm(p, ob[:, c0:c0 + 32], vsl, sm[:, c4 * 32:(c4 + 1) * 32], True, False, [vkey, smk], [ok])
                        mm(p, ob[:, c0:c0 + 32], stbf[h][:], q_in[hl][:, c0:c0 + 32], False, True,
                           [("stbf", h), ("q_in", hl)], [ok])
                    for hl in range(4):
                        h = hp * 4 + hl
                        kv, kk2 = kvs[hl]
                        p.dve(lambda e, kv=kv, h=h, hl=hl, cc=cc: e.scalar_tensor_tensor(
                            out=state[h][:], in0=state[h][:], scalar=dcy[hl][:, cc:cc + 1], in1=kv,
                            op0=ALU.mult, op1=ALU.add), r=[kk2, ("state", h), ("dcy", hl)], w=[("state", h)])
                        p.act(lambda e, h=h: e.activation(out=stbf[h][:], in_=state[h][:], func=AF.Copy),
                              r=[("state", h)], w=[("stbf", h)])
            if DBG == 3:
                continue
            for hl in range(4):
                h = hp * 4 + hl
                ob, ok = st.PS[hl], ("ps", hl)
                p.act(lambda e, ob=ob: e.activation(out=o_sb[:], in_=ob[:], func=AF.Copy), r=[ok], w=["o_sb"])
                p.pool(lambda e: e.tensor_tensor(out=sq[:], in0=o_sb[:], in1=o_sb[:], op=ALU.mult),
                       r=["o_sb"], w=["sq"])
                vb, vk = st.bank(RS)
                mm(p, vb[:], onesd[:], sq[:], True, True, ["onesd", "sq"], [vk])
                p.act(lambda e, vb=vb: e.activation(out=sd[:], in_=vb[:], func=AF.Sqrt, bias=st.epsb[:], scale=1.0),
                      r=[vk, "epsb"], w=["sd"])
                p.dve(lambda e: e.reciprocal(out=sd[:], in_=sd[:]), r=["sd"], w=["sd"])
                p.pool(lambda e: e.tensor_tensor(out=o_sb[:], in0=o_sb[:], in1=sd[:], op=ALU.mult),
                       r=["o_sb", "sd"], w=["o_sb"])
                p.dve(lambda e, h=h, hl=hl: e.scalar_tensor_tensor(out=yT[h][:], in0=o_sb[:], scalar=cg[:, h:h + 1],
                                                                   in1=sgT[hl][:], op0=ALU.mult, op1=ALU.mult),
                      r=["o_sb", "cg", ("sgT", hl)], w=[("yT", h)])
        if DBG in (2, 3, 21, 22, 23, 24):
            continue
        for i in range(4):
            mix = []
            mkeys = []
            for hf in range(2):
                mb, mk = st.bank(R4)
                for f in range(8):
                    mm(p, mb[:], yT[f][:, i * 128:(i + 1) * 128], wout[:, f, hf * 512:(hf + 1) * 512],
                       f == 0, f == 7, [("yT", f), "wout"], [mk])
                mix.append(mb[:])
                mkeys.append(mk)
            ln_epilogue(st, mix, mkeys, xt[:, i, :], [("c", "xt", i)],
                        dst[B * 512 + i * 128:B * 512 + (i + 1) * 128, :], ("c", B, i))
    p.flush()


def stage_peer(p, dram, src, dst, S, l):
    st = Stage(p, dram, banks=False)
    alloc_ln(st)
    st.load_ln(2 * l + 1)
    ngrp = S // 256
    PSall = p.ps([128, 4096], F32)

    def bk(b, n=1):
        return PSall[:, b * 512:(b + n) * 512]

    st.PS = [bk(b) for b in range(8)]
    WQ = dram[f"p{l}_wq"].rearrange("(k p) n -> p k n", p=128)
    keysT = p.sb([128, 16, 128], BF16)
    p.dma("pool", keysT[:], dram[f"p{l}_keysT"], w=["keysT"])
    bmask = p.sb([128, 8, 16], BF16)
    iota = p.sb([128, 128, 8], BF16)
    p.dma("sp", bmask[:], dram["p_bmask"].rearrange("p (h a) -> p h a", a=16), w=["bmask"])
    p.dma("sp", iota[:], dram["p_iota"], w=["iota"])
    UT = dram[f"p{l}_ut"]
    VV = dram[f"p{l}_v"]
    UT16 = dram[f"p{l}_ut16"]
    V16 = dram[f"p{l}_v16"]

    xt = [p.sb([128, 2, 1024], F32) for _ in range(2)]
    xb = p.sb([128, 1, 1024], BF16)
    hT = [p.sb([128, 8, 256], BF16) for _ in range(2)]
    wqb = [p.sb([128, 8, 128], BF16) for _ in range(3)]
    qT_sb = p.sb([128, 16, 256], BF16)
    s_sb = p.sb([128, 16, 128], F32)
    s2 = p.sb([128, 16, 128], F32)
    v16 = p.sb([128, 16, 16], F32)
    i16 = p.sb([128, 16, 16], U32)
    idxf = p.sb([128, 2, 8, 16], BF16)
    cand = p.sb([128, 8, 256], F32)
    cand2 = s2[:].rearrange("p (h two) k -> p h (two k)", two=2)
    m8 = p.sb([128, 8, 16], F32)
    ex = p.sb([128, 8, 256], F32)
    msk = s_sb[:].rearrange("p (h two) k -> p h (two k)", two=2)
    zsum = p.sb([128, 8], F32)
    Gc = p.sb([128, 16, 8, 16], BF16)
    idxT = [p.sb([128, 2, 128], BF16) for _ in range(2)]
    GcT = [p.sb([128, 128, 16], BF16) for _ in range(2)]
    P1 = [p.sb([128, 128, 8], BF16) for _ in range(2)]
    P2 = [p.sb([128, 128, 8], BF16) for _ in range(2)]
    BDT = [p.sb([128, 8, 128], BF16) for _ in range(2)]
    M_sb = [p.sb([128, 8, 128], BF16) for _ in range(2)]
    GT = p.sb([128, 128, 256], BF16)
    NSL = 4
    utb = [p.sb([128, 8, 128], BF16) for _ in range(NSL)]
    vb = [p.sb([128, 1024], BF16) for _ in range(NSL)]
    ge = [p.sb([128, 256], F32) for _ in range(2)]
    cd = [p.sb([128, 256], BF16) for _ in range(2)]
    R67 = [6, 7]
    if DBG == 99:
        print("SBUF free stage_peer", p.nc.sbuf_bytes_remaining)

    def emit_A1(G):
        par = G % 2
        tag = ("pe", par)
        hTk = [(tag, "hT", i) for i in range(2)]
        load_transpose(st, src[G * 256:(G + 1) * 256, :], 2, xt[par], xb, hT[par], tag, R67)
        for hp in range(16):
            ws = hp % 3
            p.dma("pool", wqb[ws][:], WQ[:, :, hp * 128:(hp + 1) * 128], w=[("wqb", ws)])
            qb, qk = st.bank(R67)
            for k in range(8):
                mm(p, qb[:, 0:256], wqb[ws][:, k, :], hT[par][:, k, :], k == 0, k == 7,
                   [("wqb", ws)] + hTk, [qk])
            p.act(lambda e, qb=qb, hp=hp: e.activation(out=qT_sb[:, hp, :], in_=qb[:, 0:256], func=AF.Copy),
                  r=[qk], w=[("qT", hp)])
        for i in range(2):
            for b4 in range(4):
                sb_, sk_ = st.bank(R67)
                for q4 in range(4):
                    hp = b4 * 4 + q4
                    mm(p, sb_[:, q4 * 128:(q4 + 1) * 128], qT_sb[:, hp, i * 128:(i + 1) * 128],
                       keysT[:, hp, :], True, True, [("qT", hp), "keysT"], [sk_])
                p.act(lambda e, b4=b4, sb_=sb_: e.activation(
                    out=s_sb[:, 4 * b4:4 * b4 + 4, :].rearrange("p a k -> p (a k)"), in_=sb_, func=AF.Copy),
                    r=[sk_], w=[("s_sb", b4)])
            for g in range(16):
                sk = ("s_sb", g // 4)
                p.dve(lambda e, g=g: e.max(out=v16[:, g, 0:8], in_=s_sb[:, g, :]), r=[sk], w=[("v16", g)])
                p.dve(lambda e, g=g: e.max_index(out=i16[:, g, 0:8], in_max=v16[:, g, 0:8], in_values=s_sb[:, g, :]),
                      r=[sk, ("v16", g)], w=[("i16", g)])
                p.dve(lambda e, g=g: e.match_replace(out=s2[:, g, :], in_to_replace=v16[:, g, 0:8],
                                                     in_values=s_sb[:, g, :], imm_value=NEG),
                      r=[sk, ("v16", g)], w=[("s2", g)])
                p.dve(lambda e, g=g: e.max(out=v16[:, g, 8:16], in_=s2[:, g, :]), r=[("s2", g)], w=[("v16", g)])
                p.dve(lambda e, g=g: e.max_index(out=i16[:, g, 8:16], in_max=v16[:, g, 8:16], in_values=s2[:, g, :]),
                      r=[("s2", g), ("v16", g)], w=[("i16", g)])
            vk = [("v16", g) for g in range(16)]
            ik = [("i16", g) for g in range(16)]
            p.dve(lambda e: e.tensor_copy(out=idxf[:].rearrange("p two h a -> p h two a"),
                                          in_=i16[:].rearrange("p (h two) a -> p h two a", two=2)),
                  r=ik, w=["idxf"])
            v4 = v16[:].rearrange("p (h two) a -> p h two a", two=2)
            c4 = cand[:].rearrange("p h (a b) -> p h a b", b=16)
            p.dve(lambda e, v4=v4, c4=c4: e.tensor_tensor(
                out=c4, in0=v4[:, :, 0, :].unsqueeze(3).broadcast_to([128, 8, 16, 16]),
                in1=v4[:, :, 1, :].unsqueeze(2).broadcast_to([128, 8, 16, 16]), op=ALU.add), r=vk, w=["cand"])
            for h in range(8):
                p.dve(lambda e, h=h: e.max(out=m8[:, h, 0:8], in_=cand[:, h, :]), r=["cand"], w=[("m8", h)])
                p.dve(lambda e, h=h: e.match_replace(out=cand2[:, h, :], in_to_replace=m8[:, h, 0:8],
                                                     in_values=cand[:, h, :], imm_value=NEG),
                      r=["cand", ("m8", h)], w=[("cand2", h), ("s2", 2 * h), ("s2", 2 * h + 1)])
                p.dve(lambda e, h=h: e.max(out=m8[:, h, 8:16], in_=cand2[:, h, :]), r=[("cand2", h)], w=[("m8", h)])
            mk = [("m8", h) for h in range(8)]
            p.dve(lambda e: e.tensor_tensor(out=ex[:], in0=cand[:], in1=m8[:, :, 0:1].broadcast_to([128, 8, 256]),
                                            op=ALU.subtract), r=["cand"] + mk, w=["ex"])
            p.act(lambda e: e.activation(out=ex[:], in_=ex[:], func=AF.Exp), r=["ex"], w=["ex"])
            p.dve(lambda e: e.tensor_tensor(out=msk, in0=cand[:], in1=m8[:, :, 15:16].broadcast_to([128, 8, 256]),
                                            op=ALU.is_ge), r=["cand"] + mk,
                  w=["msk"] + [("s_sb", b) for b in range(4)])
            p.pool(lambda e: e.tensor_tensor(out=ex[:], in0=ex[:], in1=msk, op=ALU.mult), r=["ex", "msk"], w=["ex"])
            p.dve(lambda e: e.tensor_reduce(out=zsum[:], in_=ex[:], axis=AX.X, op=ALU.add), r=["ex"], w=["zsum"])
            p.dve(lambda e: e.reciprocal(out=zsum[:], in_=zsum[:]), r=["zsum"], w=["zsum"])
            p.dve(lambda e: e.tensor_tensor(
                out=Gc[:].rearrange("p a h b -> p h a b"), in0=ex[:].rearrange("p h (a b) -> p h a b", b=16),
                in1=zsum[:].unsqueeze(2).unsqueeze(3).broadcast_to([128, 8, 16, 16]), op=ALU.mult),
                r=["ex", "zsum"], w=["Gc"])
            tbk, tkey = st.bank(R67)
            tb = tbk.bitcast(BF16)
            for two in range(2):
                p.pe(lambda e, two=two, tb=tb: e.transpose(
                    out=tb[:, two * 128:(two + 1) * 128], in_=idxf[:, two, :, :].rearrange("p h a -> p (h a)"),
                    identity=st.ident[:]),
                    r=["idxf", "ident"], w=[tkey])
            p.act(lambda e, tb=tb, i=i: e.activation(out=idxT[i][:].rearrange("p a t -> p (a t)"), in_=tb[:, 0:256],
                                                     func=AF.Copy), r=[tkey], w=[("idxT", i)])
            for half in range(2):
                gbk, gkey = st.bank(R67)
                gb = gbk.bitcast(BF16).rearrange("p (a t) -> p a t", t=128)
                for a8 in range(8):
                    a = half * 8 + a8
                    p.pe(lambda e, a=a, a8=a8, gb=gb: e.transpose(
                        out=gb[:, a8, :], in_=Gc[:, a, :, :].rearrange("p h b -> p (h b)"), identity=st.ident[:]),
                        r=["Gc", "ident"], w=[gkey])
                p.act(lambda e, gb=gb, half=half, i=i: e.activation(
                    out=GcT[i][:].rearrange("p t a -> p a t")[:, half * 8:(half + 1) * 8, :], in_=gb, func=AF.Copy),
                    r=[gkey], w=[("GcT", i, half)])

    def emit_A2(G):
        steps = [(i, sbk) for i in range(2) for sbk in range(16)]

        def banks(n):
            pb = n % 2
            mb0 = 4 if pb == 0 else 0
            gb0 = 6 if pb == 0 else 2
            return pb, mb0, gb0

        def part_m(n):
            i, sbk = steps[n]
            t0 = sbk * 8
            pb, mb0, gb0 = banks(n)
            mkeys2 = [("ps", mb0), ("ps", mb0 + 1)]
            P2_, BDT_, M_ = P2[pb], BDT[pb], M_sb[pb]
            p.dve(lambda e: e.tensor_tensor(
                out=P2_[:], in0=iota[:],
                in1=idxT[i][:, 1, t0:t0 + 8].unsqueeze(1).broadcast_to([128, 128, 8]), op=ALU.is_equal),
                r=["iota", ("idxT", i)], w=[("P2", pb)])
            p.dve(lambda e: e.tensor_tensor(
                out=BDT_[:].rearrange("p t (h a) -> p t h a", a=16),
                in0=GcT[i][:, t0:t0 + 8, :].unsqueeze(2).broadcast_to([128, 8, 8, 16]),
                in1=bmask[:].unsqueeze(1).broadcast_to([128, 8, 8, 16]), op=ALU.mult),
                r=[("GcT", i, 0), ("GcT", i, 1), "bmask"], w=[("BDT", pb)])
            for t in range(8):
                mm(p, bk(mb0, 2)[:, t * 128:(t + 1) * 128], BDT_[:, t, :], P2_[:, :, t], True, True,
                   [("BDT", pb), ("P2", pb)], mkeys2)
            p.act(lambda e: e.activation(out=M_[:].rearrange("p t j -> p (t j)"), in_=bk(mb0, 2), func=AF.Copy),
                  r=mkeys2, w=[("M_sb", pb)])

        def part_g(n):
            i, sbk = steps[n]
            t0 = sbk * 8
            pb, mb0, gb0 = banks(n)
            gkeys2 = [("ps", gb0), ("ps", gb0 + 1)]
            P1_, M_ = P1[pb], M_sb[pb]
            p.dve(lambda e: e.tensor_tensor(
                out=P1_[:], in0=iota[:],
                in1=idxT[i][:, 0, t0:t0 + 8].unsqueeze(1).broadcast_to([128, 128, 8]), op=ALU.is_equal),
                r=["iota", ("idxT", i)], w=[("P1", pb)])
            for t in range(8):
                mm(p, bk(gb0, 2)[:, t * 128:(t + 1) * 128], P1_[:, :, t], M_[:, t, :], True, True,
                   [("P1", pb), ("M_sb", pb)], gkeys2)
            gsrc = bk(gb0, 2).rearrange("p (t j) -> p j t", j=128)
            gdst = GT[:, :, i * 128 + t0:i * 128 + t0 + 8]
            if n % 3 == 2:
                p.dve(lambda e: e.tensor_copy(out=gdst, in_=gsrc), r=gkeys2, w=[("GT", i)])
            else:
                p.act(lambda e: e.activation(out=gdst, in_=gsrc, func=AF.Copy), r=gkeys2, w=[("GT", i)])

        part_m(0)
        for n in range(len(steps)):
            if n + 1 < len(steps):
                part_m(n + 1)
            part_g(n)

    def emit_B(G, extra):
        par = G % 2
        tag = ("pe", par)
        hTk = [(tag, "hT", i) for i in range(2)]
        nex = len(extra)
        done = 0
        def v_part(j):
            sl = j % NSL
            g2 = j % 2
            for i in range(2):
                for hf in range(2):
                    b = 2 * i + hf
                    mm(p, bk(b), cd[g2][:, i * 128:(i + 1) * 128], vb[sl][:, hf * 512:(hf + 1) * 512],
                       j == 0, j == 127, [("cd", g2), ("vb", sl)], [("ps", b)])

        for j in range(128):
            sl = j % NSL
            if G == 0:
                p.dma("pool", utb[sl][:], UT[j], w=[("utb", sl)])
                p.dma("pool", vb[sl][:], VV[j], w=[("vb", sl)])
                p.dma("sp", UT16[j], utb[sl][:], r=[("utb", sl)], w=[("ut16", j)])
                p.dma("sp", V16[j], vb[sl][:], r=[("vb", sl)], w=[("v16d", j)])
            else:
                p.dma("sp", utb[sl][:], UT16[j], r=[("ut16", j)], w=[("utb", sl)])
                p.dma("act", vb[sl][:], V16[j], r=[("v16d", j)], w=[("vb", sl)])
            ab = 4 + j % 2
            for k in range(8):
                mm(p, bk(ab)[:, 0:256], utb[sl][:, k, :], hT[par][:, k, :], k == 0, k == 7,
                   [("utb", sl)] + hTk, [("ps", ab)])
            g2 = j % 2
            p.act(lambda e, ab=ab, g2=g2: e.activation(out=ge[g2][:], in_=bk(ab)[:, 0:256], func=AF.Gelu),
                  r=[("ps", ab)], w=[("ge", g2)])
            p.dve(lambda e, g2=g2, j=j: e.tensor_tensor(out=cd[g2][:], in0=ge[g2][:], in1=GT[:, j, :], op=ALU.mult),
                  r=[("ge", g2), ("GT", 0), ("GT", 1)], w=[("cd", g2)])
            if j > 0:
                v_part(j - 1)
            upto = (nex * (j + 1)) // 120 if j < 119 else nex
            upto = min(upto, nex)
            if upto > done:
                p.ops.extend(extra[done:upto])
                done = upto
        v_part(127)
        for i in range(2):
            ln_epilogue(st, [bk(2 * i), bk(2 * i + 1)], [("ps", 2 * i), ("ps", 2 * i + 1)], xt[par][:, i, :],
                        [(tag, "xt", i)], dst[G * 256 + i * 128:G * 256 + (i + 1) * 128, :], ("pe", G, i))

    def capture(fn, *a):
        saved = p.ops
        p.ops = []
        fn(*a)
        got = p.ops
        p.ops = saved
        return got

    emit_A1(0)
    for G in range(ngrp):
        emit_A2(G)
        extra = capture(emit_A1, G + 1) if G + 1 < ngrp else []
        emit_B(G, extra)
    p.flush()


def _bf(a):
    return np.ascontiguousarray(a).astype(ml_dtypes.bfloat16)


def _consts(S):
    c = {}
    c["ident"] = _bf(np.eye(128, dtype=np.float32))
    half = 32
    inv = 1.0 / (10000.0 ** np.linspace(0.0, 1.0, half, dtype=np.float32)).astype(np.float32)
    pos = np.arange(S, dtype=np.float32)
    ang = (pos[None, :] * inv[:, None]).astype(np.float32)
    cos = np.cos(ang).astype(np.float32)
    sin = np.sin(ang).astype(np.float32)
    c["ab_cos"] = np.ascontiguousarray(np.concatenate([cos, cos], 0))
    c["ab_sin"] = np.ascontiguousarray(np.concatenate([-sin, sin], 0))
    idx = np.arange(128, dtype=np.float64)
    dq = np.zeros((64, 4, 128), np.float32)
    dk = np.zeros((64, 4, 128), np.float32)
    for h in range(4):
        lg = math.log1p(-2.0 ** (-5.0 - h))
        dq[:, h, :] = np.exp(lg * (idx + 1.0))[None, :]
        dk[:, h, :] = (np.exp(-lg * (idx + 1.0)) * (64 ** -0.5))[None, :]
    c["ab_dq"] = dq
    c["ab_dk"] = dk
    s_ = np.arange(128)[:, None]
    t_ = np.arange(128)[None, :]
    c["ab_maskT"] = (s_ <= t_).astype(np.float32)
    pm = np.zeros((128, 4, 3, 128), np.float32)
    for g, w in enumerate((2, 4, 8, 16)):
        cnt = np.minimum(t_ + 1, w).astype(np.float32)
        inwin = (s_ <= t_) & (s_ > t_ - w)
        pm[:, g, 0, :] = inwin / cnt - (s_ == t_)
        pm[:, g, 1, :] = inwin / float(w) - (s_ == t_)
        pm[:, g, 2, :] = ((s_ - 128) > (t_ - w)) / float(w)
    c["ab_pm"] = _bf(pm)
    c["c_m96"] = (np.arange(128) >= 96).astype(np.float32).reshape(128, 1)
    c["c_maskC"] = ((s_ <= t_) & ((s_ // 32) == (t_ // 32))).astype(np.float32)
    hh = np.arange(128) // 16
    c["p_bmask"] = _bf((hh[:, None] == hh[None, :]).astype(np.float32))
    c["p_iota"] = _bf(np.broadcast_to(np.arange(128, dtype=np.float32)[None, :, None], (128, 128, 8)))
    return c


def _prep_weights(inp):
    w = {}
    ab = np.asarray(inp["ab_w_in"])[0]
    cols = []
    swp = np.concatenate([np.arange(32, 64), np.arange(0, 32)])
    for h in range(4):
        q0 = 512 + h * 64
        k0 = 768 + h * 64
        cols += [np.arange(q0, q0 + 64), q0 + swp, np.arange(k0, k0 + 64), k0 + swp]
    cols.append(np.arange(1536, 2048))
    w["ab_wfm"] = np.ascontiguousarray(ab[:, np.concatenate(cols)])
    w["ab_wtm"] = np.ascontiguousarray(ab[:, np.concatenate([np.arange(0, 512), np.arange(1024, 1536)])])
    w["ab_wout"] = np.ascontiguousarray(np.asarray(inp["ab_w_out"])[0])
    w["ab_poolw"] = np.ascontiguousarray(np.asarray(inp["pool_w"])[0].transpose(1, 0, 2))
    w["ab_pscale"] = np.ascontiguousarray(np.asarray(inp["pool_scale"])[0].reshape(4, 128).T)
    w["ab_retg"] = np.ascontiguousarray(np.asarray(inp["ret_norm_g"])[0].reshape(4, 128).T)
    cw = np.asarray(inp["c_w_in"])[0]
    w["c_wfm"] = np.ascontiguousarray(cw[:, np.concatenate([np.arange(0, 2048), np.arange(3072, 4096)])])
    w["c_wtm"] = np.ascontiguousarray(cw[:, 2048:3072])
    w["c_wout"] = np.ascontiguousarray(np.asarray(inp["c_w_out"])[0])
    w["c_lb"] = np.ascontiguousarray(np.asarray(inp["hgrn_lb"]).reshape(2, 8, 128).transpose(2, 0, 1))
    w["c_g"] = np.ascontiguousarray(np.asarray(inp["hgrn_norm_g"])[0].reshape(8, 128).T)
    lng = np.asarray(inp["ln_g"]).reshape(4, 1, 1024)
    lnb = np.asarray(inp["ln_b"]).reshape(4, 1, 1024)
    w["lng"] = np.ascontiguousarray(np.broadcast_to(lng, (4, 128, 1024)))
    w["lnb"] = np.ascontiguousarray(np.broadcast_to(lnb, (4, 128, 1024)))
    for l in range(2):
        w[f"p{l}_wq"] = np.ascontiguousarray(np.asarray(inp["peer_w_q"])[l])
        sk = np.asarray(inp["peer_sub_keys"])[l]
        w[f"p{l}_keysT"] = np.ascontiguousarray(sk.transpose(3, 0, 1, 2).reshape(128, 16, 128))
        u = np.asarray(inp["peer_u"])[l]
        w[f"p{l}_ut"] = np.ascontiguousarray(u.reshape(128, 128, 8, 128).transpose(1, 3, 2, 0))
        v = np.asarray(inp["peer_v"])[l]
        w[f"p{l}_v"] = np.ascontiguousarray(v.reshape(128, 128, 1024).transpose(1, 0, 2))
    return w


_DT = {np.dtype(np.float32): F32, np.dtype(ml_dtypes.bfloat16): BF16}


def build(S, stages, shapes):
    nc = bass.Bass("TRN2", target_bir_lowering=False)
    dram = {}
    for name, (shp, dt) in shapes.items():
        dram[name] = nc.dram_tensor(name, list(shp), _DT[np.dtype(dt)], kind="ExternalInput").ap()
    x = nc.dram_tensor("x", [S, D], F32, kind="ExternalInput").ap()
    out = nc.dram_tensor("out", [S, D], F32, kind="ExternalOutput").ap()
    p = Prog(nc)
    for sname in stages:
        if sname.startswith("peer"):
            l = sname[-1]
            dram[f"p{l}_ut16"] = nc.dram_tensor(f"p{l}_ut16", [128, 128, 8, 128], BF16, kind="Internal").ap()
            dram[f"p{l}_v16"] = nc.dram_tensor(f"p{l}_v16", [128, 128, 1024], BF16, kind="Internal").ap()
    cur = x
    for si, sname in enumerate(stages):
        if si == len(stages) - 1:
            nxt = out
        else:
            nxt = nc.dram_tensor(f"hbuf{si}", [S, D], F32, kind="Internal").ap()
        if sname == "ab":
            stage_ab(p, dram, cur, nxt, S)
        elif sname == "c":
            stage_c(p, dram, cur, nxt, S)
        elif sname in ("peer0", "peer1"):
            stage_peer(p, dram, cur, nxt, S, int(sname[-1]))
        cur = nxt
    p.finalize()
    return nc


def stage_inputs(stages):
    need = {"ident", "lng", "lnb"}
    for s in stages:
        if s == "ab":
            need |= {"ab_wfm", "ab_wtm", "ab_wout", "ab_poolw", "ab_pscale", "ab_retg", "ab_cos", "ab_sin",
                     "ab_dq", "ab_dk", "ab_maskT", "ab_pm"}
        elif s == "c":
            need |= {"c_wfm", "c_wtm", "c_wout", "c_lb", "c_g", "c_maskC", "c_m96"}
        else:
            l = s[-1]
            need |= {f"p{l}_wq", f"p{l}_keysT", f"p{l}_ut", f"p{l}_v", "p_bmask", "p_iota"}
    return need


STAGES = ["ab", "peer0", "c", "peer1"]


def kernel(**inputs):
    S = 4096
    shared = dict(_consts(S))
    shared.update(_prep_weights(inputs))
    shapes = {k: (v.shape, v.dtype) for k, v in shared.items()}
    nc = build(S, STAGES, shapes)
    x = np.asarray(inputs["x"])
    in_maps = [dict(shared, x=np.ascontiguousarray(x[b])) for b in range(8)]
    res = run_bass_kernel_spmd(nc, in_maps, core_ids=list(range(8)))
    return np.stack([np.asarray(r["out"]) for r in res.results], 0).astype(np.float32)
```

```python
import math
import numpy as np
import ml_dtypes
from contextlib import ExitStack
import concourse.bass as bass
import concourse.mybir as mybir
from concourse.bass_utils import run_bass_kernel_spmd

F32 = mybir.dt.float32
BF16 = mybir.dt.bfloat16
U32 = mybir.dt.uint32
ALU = mybir.AluOpType
AF = mybir.ActivationFunctionType
AX = mybir.AxisListType

ENGS = ("pe", "act", "dve", "pool", "sp")
NDMA_SEM = 6

D = 1024
ALPHA = float((2 * 2) ** 0.25)
EPS = 1e-5
NEG = -1.0e30
import os
DBG = int(os.environ.get('KDBG', '0'))


class Op:
    __slots__ = ("eng", "fn", "reads", "writes", "dma", "deps", "needs_inc", "tok", "waits")

    def __init__(self, eng, fn, reads, writes, dma):
        self.eng = eng
        self.fn = fn
        self.reads = reads
        self.writes = writes
        self.dma = dma
        self.deps = ()
        self.needs_inc = False
        self.tok = None
        self.waits = ()


class Prog:
    def __init__(self, nc):
        self.nc = nc
        self.ops = []
        self.gstack = ExitStack()
        self.stack = ExitStack()
        self._n = 0
        st = self.gstack
        self.esem = {e: st.enter_context(nc.semaphore(f"s_{e}")) for e in ENGS}
        self.dsem = {e: [st.enter_context(nc.semaphore(f"d_{e}{i}")) for i in range(NDMA_SEM)]
                     for e in ("sp", "act", "pool")}
        self.ecnt = {e: 0 for e in ENGS}
        self.dcnt = {e: [0] * NDMA_SEM for e in self.dsem}
        self.drr = {e: 0 for e in self.dsem}
        self.waited = {e: {} for e in ENGS}
        self.counts = {e: 0 for e in ENGS}

    def sb(self, shape, dtype, name=None):
        self._n += 1
        return self.stack.enter_context(
            self.nc.sbuf_tensor(name or f"sb{self._n}", list(shape), dtype))

    def ps(self, shape, dtype, name=None):
        self._n += 1
        return self.stack.enter_context(
            self.nc.psum_tensor(name or f"ps{self._n}", list(shape), dtype))

    def add(self, eng, fn, reads=(), writes=(), dma=False):
        o = Op(eng, fn, tuple(reads), tuple(writes), dma)
        self.ops.append(o)
        return o

    def pe(self, fn, r=(), w=()):
        return self.add("pe", fn, r, w)

    def act(self, fn, r=(), w=()):
        return self.add("act", fn, r, w)

    def dve(self, fn, r=(), w=()):
        return self.add("dve", fn, r, w)

    def pool(self, fn, r=(), w=()):
        return self.add("pool", fn, r, w)

    def dma(self, q, out, in_, r=(), w=(), **kw):
        return self.add(q, lambda e: e.dma_start(out=out, in_=in_, **kw), r, w, dma=True)

    def barrier(self):
        for e in ENGS:
            o = self.add(e, None, (), ())
            o.dma = "barrier"

    def finalize(self):
        self.flush()
        self.gstack.close()
        return self.counts

    def flush(self):
        self.barrier()
        nc = self.nc
        ops = self.ops
        last_w = {}
        readers = {}
        dependents = [False] * len(ops)
        last_real = {}
        for i, o in enumerate(ops):
            if o.dma == "barrier":
                for j in last_real.values():
                    dependents[j] = True
                last_w.clear()
                readers.clear()
                continue
            deps = set()
            for k in o.reads:
                j = last_w.get(k)
                if j is not None:
                    deps.add(j)
            for k in o.writes:
                j = last_w.get(k)
                if j is not None:
                    deps.add(j)
                rd = readers.get(k)
                if rd:
                    deps.update(rd[0].values())
                    deps.update(rd[1])
            deps.discard(i)
            if o.eng == "pe" and not o.dma:
                deps = {j for j in deps if not (ops[j].eng == "pe" and not ops[j].dma)}
            o.deps = sorted(deps)
            for j in o.deps:
                dependents[j] = True
            for k in o.writes:
                last_w[k] = i
                readers[k] = None
            for k in o.reads:
                if k not in o.writes:
                    rd = readers.get(k)
                    if rd is None:
                        rd = readers[k] = ({}, [])
                    if o.dma:
                        rd[1].append(i)
                    else:
                        rd[0][o.eng] = i
            if not o.dma and o.fn is not None:
                last_real[o.eng] = i
        esem, dsem, ecnt, dcnt, drr, waited = (self.esem, self.dsem, self.ecnt, self.dcnt,
                                               self.drr, self.waited)
        for i, o in enumerate(ops):
            ws = []
            w = waited[o.eng]

            def need(tok):
                if tok is None:
                    return
                sem, val = tok
                key = id(sem)
                if w.get(key, 0) >= val:
                    return
                w[key] = val
                ws.append((sem, val))

            if o.dma == "barrier":
                for e2 in ENGS:
                    if ecnt[e2] > 0:
                        need((esem[e2], ecnt[e2]))
                for q in dsem:
                    for s in range(NDMA_SEM):
                        if dcnt[q][s] > 0:
                            need((dsem[q][s], dcnt[q][s]))
                o.waits = ws
                continue
            for j in o.deps:
                need(ops[j].tok)
            if o.dma:
                s = drr[o.eng] % NDMA_SEM
                drr[o.eng] += 1
                prev = dcnt[o.eng][s]
                if prev > 0:
                    need((dsem[o.eng][s], prev))
                dcnt[o.eng][s] = prev + 16
                o.tok = (dsem[o.eng][s], prev + 16)
                o.needs_inc = True
            elif o.fn is not None and dependents[i]:
                ecnt[o.eng] += 1
                o.tok = (esem[o.eng], ecnt[o.eng])
                o.needs_inc = True
            o.waits = ws
        by_eng = {e: [o for o in ops if o.eng == e] for e in ENGS}

        def emit(eng_obj, lst):
            for o in lst:
                for sem, val in o.waits:
                    eng_obj.wait_ge(sem, val)
                if o.fn is None:
                    continue
                ins = o.fn(eng_obj)
                if o.needs_inc:
                    ins.then_inc(o.tok[0], 16 if o.dma else 1)

        with nc.Block() as block:
            @block.sync
            def _(e):
                emit(e, by_eng["sp"])

            @block.scalar
            def _(e):
                emit(e, by_eng["act"])

            @block.vector
            def _(e):
                emit(e, by_eng["dve"])

            @block.gpsimd
            def _(e):
                emit(e, by_eng["pool"])

            @block.tensor
            def _(e):
                emit(e, by_eng["pe"])
        self.stack.close()
        self.stack = ExitStack()
        self.ops = []
        for e in ENGS:
            self.counts[e] += len(by_eng[e])


class Stage:
    def __init__(self, p, dram, banks=True):
        self.p = p
        self.dram = dram
        if banks:
            self.PS = [p.ps([128, 512], F32, name=f"bank{p._n}_{i}") for i in range(8)]
        self.rr = {}
        self.ident = p.sb([128, 128], BF16)
        p.dma("sp", self.ident[:], dram["ident"], w=["ident"])
        self.epsb = p.sb([128, 1], F32)
        p.dve(lambda e: e.memset(self.epsb[:], EPS), w=["epsb"])
        self.lng = p.sb([128, 1024], F32)
        self.lnb = p.sb([128, 1024], F32)

    def load_ln(self, li):
        self.p.dma("sp", self.lng[:], self.dram["lng"][li], w=["lng"])
        self.p.dma("sp", self.lnb[:], self.dram["lnb"][li], w=["lnb"])

    def bank(self, pool):
        pk = tuple(pool)
        i = self.rr.get(pk, 0)
        self.rr[pk] = i + 1
        b = pool[i % len(pool)]
        return self.PS[b], ("ps", b)


def capture_ops(p, fn, *a):
    saved = p.ops
    p.ops = []
    fn(*a)
    got = p.ops
    p.ops = saved
    return got


def mm(p, out, lhsT, rhs, start, stop, r, w):
    p.pe(lambda e: e.matmul(out, lhsT=lhsT, rhs=rhs, start=start, stop=stop), r=r, w=w)


def load_transpose(st, src, ntile, xt, xb, hT, tag, banks):
    p = st.p
    nb = xb.shape[1]
    for i in range(ntile):
        ib = i % nb
        p.dma("sp", xt[:, i, :], src[i * 128:(i + 1) * 128, :], w=[(tag, "xt", i)])
        p.act(lambda e, i=i, ib=ib: e.activation(out=xb[:, ib, :], in_=xt[:, i, :], func=AF.Copy),
              r=[(tag, "xt", i)], w=[(tag, "xb", ib)])
        bk, bkey = st.bank(banks)
        bv = bk[:].bitcast(BF16).rearrange("p (k n) -> p k n", n=128)
        for k in range(8):
            p.pe(lambda e, ib=ib, k=k, bv=bv: e.transpose(out=bv[:, k, :], in_=xb[:, ib, k * 128:(k + 1) * 128],
                                                          identity=st.ident[:]),
                 r=[(tag, "xb", ib), "ident"], w=[bkey])
        p.dve(lambda e, i=i, bv=bv: e.tensor_copy(out=hT[:, :, i * 128:(i + 1) * 128], in_=bv),
              r=[bkey], w=[(tag, "hT", i)])


def ln_epilogue(st, mix, mixkeys, xt_ap, xkeys, dst, tag):
    p = st.p
    z = st.z
    for hf in range(2):
        p.dve(lambda e, hf=hf: e.scalar_tensor_tensor(out=z[:, hf * 512:(hf + 1) * 512],
                                                      in0=xt_ap[:, hf * 512:(hf + 1) * 512], scalar=ALPHA,
                                                      in1=mix[hf], op0=ALU.mult, op1=ALU.add),
              r=[mixkeys[hf]] + list(xkeys), w=[("z", hf)])
        p.dve(lambda e, hf=hf: e.bn_stats(out=st.stats[:, hf, :], in_=z[:, hf * 512:(hf + 1) * 512]),
              r=[("z", hf)], w=[("stats", hf)])
    p.dve(lambda e: e.bn_aggr(out=st.mv[:], in_=st.stats[:].rearrange("p a b -> p (a b)")),
          r=[("stats", 0), ("stats", 1)], w=["mv"])
    p.act(lambda e: e.activation(out=st.rstd[:], in_=st.mv[:, 1:2], func=AF.Sqrt, bias=st.epsb[:], scale=1.0),
          r=["mv", "epsb"], w=["rstd0"])
    p.dve(lambda e: e.reciprocal(out=st.rstd[:], in_=st.rstd[:]), r=["rstd0"], w=["rstd0", "rstd"])
    p.dve(lambda e: e.tensor_scalar(out=z[:], in0=z[:], scalar1=st.mv[:, 0:1], scalar2=st.rstd[:],
                                    op0=ALU.subtract, op1=ALU.mult),
          r=["mv", "rstd", ("z", 0), ("z", 1)], w=["zn"])
    p.pool(lambda e: e.tensor_tensor(out=z[:], in0=z[:], in1=st.lng[:], op=ALU.mult), r=["zn", "lng"], w=["zn2"])
    p.pool(lambda e: e.tensor_tensor(out=z[:], in0=z[:], in1=st.lnb[:], op=ALU.add), r=["zn2", "lnb"], w=["zn3"])
    p.dma("sp", dst, z[:], r=["zn3"], w=[(tag, "dst"), ("z", 0), ("z", 1), "zn", "zn2", "zn3"])


def alloc_ln(st):
    p = st.p
    st.z = p.sb([128, 1024], F32)
    st.stats = p.sb([128, 2, 6], F32)
    st.mv = p.sb([128, 2], F32)
    st.rstd = p.sb([128, 1], F32)


def wload(p, dst, src, key, q="pool"):
    p.dma(q, dst[:], src.rearrange("(k p) n -> p k n", p=128), w=[key])


GAMMA = [1.0 - 2.0 ** (-5.0 - h) for h in range(4)]


def stage_ab(p, dram, src, dst, S):
    st = Stage(p, dram)
    alloc_ln(st)
    st.load_ln(0)
    nblk = S // 512
    R4 = [0, 1, 2, 3]
    wfm = p.sb([128, 8, 1536], BF16)
    wtm = p.sb([128, 8, 1024], BF16)
    wout = p.sb([128, 8, 1024], BF16)
    wload(p, wfm, dram["ab_wfm"], "wfm")
    wload(p, wtm, dram["ab_wtm"], "wtm")
    wload(p, wout, dram["ab_wout"], "wout")
    dq = p.sb([64, 4, 128], F32)
    dk = p.sb([64, 4, 128], F32)
    maskT = p.sb([128, 128], F32)
    pm = p.sb([128, 4, 3, 128], BF16)
    poolw = p.sb([128, 4, 128], BF16)
    pscale = p.sb([128, 4], F32)
    retg = p.sb([128, 4], F32)
    onesd = p.sb([128, 128], F32)
    p.dma("sp", dq[:], dram["ab_dq"], w=["dq"])
    p.dma("sp", dk[:], dram["ab_dk"], w=["dk"])
    p.dma("sp", maskT[:], dram["ab_maskT"], w=["maskT"])
    p.dma("sp", pm[:], dram["ab_pm"], w=["pm"])
    p.dma("pool", poolw[:], dram["ab_poolw"], w=["poolw"])
    p.dma("sp", pscale[:], dram["ab_pscale"], w=["pscale"])
    p.dma("sp", retg[:], dram["ab_retg"], w=["retg"])
    p.dve(lambda e: e.memset(onesd[:], 1.0 / 128.0), w=["onesd"])

    xt = p.sb([128, 4, 1024], F32)
    xb = p.sb([128, 4, 1024], BF16)
    hT = p.sb([128, 8, 512], BF16)
    cosb = p.sb([64, 512], F32)
    sinb = p.sb([64, 512], F32)
    qT = [p.sb([64, 512], BF16) for _ in range(4)]
    kT = [p.sb([64, 512], BF16) for _ in range(4)]
    sgT = p.sb([128, 4, 512], F32)
    u_tm = [p.sb([128, 4, 512], BF16) for _ in range(2)]
    v_tm = p.sb([128, 4, 512], BF16)
    ktm = p.sb([128, 4, 4, 64], BF16)
    TT = [[p.sb([64, 512], F32) for _ in range(2)] for _ in range(2)]
    scTm = [p.sb([128, 128], BF16) for _ in range(4)]
    Tst = [p.sb([64, 128], F32) for _ in range(4)]
    stbf = [p.sb([64, 128], BF16) for _ in range(4)]
    TL = [[p.sb([128, 512], F32) for _ in range(4)] for _ in range(2)]
    yT = [p.sb([128, 512], BF16) for _ in range(8)]
    pT_sb = p.sb([128, 512], BF16)
    for h in range(4):
        p.dve(lambda e, h=h: e.memset(Tst[h][:], 0.0), w=[("T", h)])
        p.dve(lambda e, h=h: e.memset(stbf[h][:], 0.0), w=[("stbf", h)])

    if DBG == 99:
        print("SBUF free stage_ab", p.nc.sbuf_bytes_remaining)
    hTk = [("ab", "hT", i) for i in range(4)]
    for B in range(nblk):
        ub = B % 2
        load_transpose(st, src[B * 512:(B + 1) * 512, :], 4, xt, xb, hT, "ab", R4)
        p.dma("sp", cosb[:], dram["ab_cos"][:, B * 512:(B + 1) * 512], w=["cosb"])
        p.dma("sp", sinb[:], dram["ab_sin"][:, B * 512:(B + 1) * 512], w=["sinb"])
        def proj_head(h):
            s_ = h % 2
            t1, t2 = TT[s_]
            RB = [0, 1] if s_ == 0 else [2, 3]
            for which in range(2):
                col0 = h * 256 + which * 128
                dst_t = (qT if which == 0 else kT)[h]
                dkey = ("qT" if which == 0 else "kT", h)
                dec = dq if which == 0 else dk
                b1, k1 = st.bank(RB)
                b2, k2 = st.bank(RB)
                for k in range(8):
                    mm(p, b1[0:64, :], wfm[:, k, col0:col0 + 64], hT[:, k, :], k == 0, k == 7,
                       ["wfm"] + hTk, [k1])
                for k in range(8):
                    mm(p, b2[0:64, :], wfm[:, k, col0 + 64:col0 + 128], hT[:, k, :], k == 0, k == 7,
                       ["wfm"] + hTk, [k2])
                p.dve(lambda e, b1=b1: e.tensor_tensor(out=t1[:], in0=b1[0:64, :], in1=cosb[:], op=ALU.mult),
                      r=[k1, "cosb"], w=[("t1", s_)])
                p.dve(lambda e, b2=b2: e.tensor_tensor(out=t2[:], in0=b2[0:64, :], in1=sinb[:], op=ALU.mult),
                      r=[k2, "sinb"], w=[("t2", s_)])
                p.pool(lambda e: e.tensor_tensor(out=t1[:], in0=t1[:], in1=t2[:], op=ALU.add),
                       r=[("t1", s_), ("t2", s_)], w=[("t1", s_)])
                p.pool(lambda e, dst_t=dst_t, dec=dec, h=h: e.tensor_tensor(
                    out=dst_t[:].rearrange("p (c j) -> p c j", j=128),
                    in0=t1[:].rearrange("p (c j) -> p c j", j=128),
                    in1=dec[:, h, :].unsqueeze(1).broadcast_to([64, 4, 128]), op=ALU.mult),
                    r=[("t1", s_), "dq", "dk"], w=[dkey])
            bg, kg = st.bank(RB)
            for k in range(8):
                mm(p, bg[:], wfm[:, k, 1024 + h * 128:1024 + (h + 1) * 128], hT[:, k, :], k == 0, k == 7,
                   ["wfm"] + hTk, [kg])
            p.act(lambda e, bg=bg, h=h: e.activation(out=sgT[:, h, :], in_=bg[:], func=AF.Silu),
                  r=[kg], w=[("sgT", h)])
        for pair in range(2):
            ops_a = capture_ops(p, proj_head, 2 * pair)
            ops_b = capture_ops(p, proj_head, 2 * pair + 1)
            kz = 0
            while kz < max(len(ops_a), len(ops_b)):
                p.ops.extend(ops_a[kz:kz + 2])
                p.ops.extend(ops_b[kz:kz + 2])
                kz += 2
        for i in range(4):
            for grp in range(2):
                bk, kk = st.bank(R4)
                for k in range(8):
                    mm(p, bk[:], hT[:, k, i * 128:(i + 1) * 128], wtm[:, k, grp * 512:(grp + 1) * 512],
                       k == 0, k == 7, ["wtm"] + hTk, [kk])
                if grp == 0:
                    p.act(lambda e, bk=bk, i=i, ub=ub: e.activation(out=u_tm[ub][:, i, :], in_=bk[:], func=AF.Copy),
                          r=[kk], w=[("u_tm", ub, i)])
                else:
                    p.act(lambda e, bk=bk, i=i: e.activation(out=v_tm[:, i, :], in_=bk[:], func=AF.Copy),
                          r=[kk], w=[("v_tm", i)])
        bk, kk = st.bank(R4)
        bv = bk[:].bitcast(BF16).rearrange("p (h i d) -> p h i d", h=4, i=4)
        for h in range(4):
            for i in range(4):
                p.pe(lambda e, h=h, i=i, bv=bv: e.transpose(out=bv[:, h, i, :], in_=kT[h][:, i * 128:(i + 1) * 128],
                                                            identity=st.ident[0:64, 0:64]),
                     r=[("kT", h), "ident"], w=[kk])
        p.act(lambda e, bv=bv: e.activation(out=ktm[:], in_=bv, func=AF.Copy), r=[kk], w=["ktm"])
        for i in range(4):
            kvb = []
            for h in range(4):
                sc, ks = st.bank(R4)
                mm(p, sc[:, 0:128], kT[h][:, i * 128:(i + 1) * 128], qT[h][:, i * 128:(i + 1) * 128], True, True,
                   [("kT", h), ("qT", h)], [ks])
                p.dve(lambda e, sc=sc, h=h: e.tensor_tensor(out=scTm[h][:], in0=sc[:, 0:128], in1=maskT[:],
                                                            op=ALU.mult),
                      r=[ks, "maskT"], w=[("scTm", h)])
            kvl = []
            for h in range(4):
                kv, kk2 = st.bank(R4)
                mm(p, kv[0:64, 0:128], ktm[:, h, i, :], v_tm[:, i, h * 128:(h + 1) * 128], True, True,
                   ["ktm", ("v_tm", i)], [kk2])
                kvl.append((kv, kk2))
            for h in range(4):
                ob, ok = st.PS[4 + h], ("ps", 4 + h)
                mm(p, ob[:, i * 128:(i + 1) * 128], v_tm[:, i, h * 128:(h + 1) * 128], scTm[h][:], True, False,
                   [("v_tm", i), ("scTm", h)], [ok])
                mm(p, ob[:, i * 128:(i + 1) * 128], stbf[h][:], qT[h][:, i * 128:(i + 1) * 128], False, True,
                   [("stbf", h), ("qT", h)], [ok])
            for h in range(4):
                kv, kk2 = kvl[h]
                gC = float(GAMMA[h] ** 128)
                p.dve(lambda e, kv=kv, h=h, gC=gC: e.scalar_tensor_tensor(
                    out=Tst[h][:], in0=Tst[h][:], scalar=gC, in1=kv[0:64, 0:128], op0=ALU.mult, op1=ALU.add),
                    r=[kk2, ("T", h)], w=[("T", h)])
                p.act(lambda e, h=h, gC=gC: e.activation(out=stbf[h][:], in_=Tst[h][:], func=AF.Copy, scale=gC),
                      r=[("T", h)], w=[("stbf", h)])
        def ln_head(h):
            s_ = h % 2
            o_sb, cen, sq, sd = TL[s_]
            RB = [0, 1] if s_ == 0 else [2, 3]
            ob, ok = st.PS[4 + h], ("ps", 4 + h)
            p.act(lambda e, ob=ob: e.activation(out=o_sb[:], in_=ob[:], func=AF.Copy), r=[ok], w=[("o_sb", s_)])
            mb, mk = st.bank(RB)
            mm(p, mb[:], onesd[:], o_sb[:], True, True, ["onesd", ("o_sb", s_)], [mk])
            p.dve(lambda e, mb=mb: e.tensor_tensor(out=cen[:], in0=o_sb[:], in1=mb[:], op=ALU.subtract),
                  r=[("o_sb", s_), mk], w=[("cen", s_)])
            p.pool(lambda e: e.tensor_tensor(out=sq[:], in0=cen[:], in1=cen[:], op=ALU.mult), r=[("cen", s_)], w=[("sq", s_)])
            vb, vk = st.bank(RB)
            mm(p, vb[:], onesd[:], sq[:], True, True, ["onesd", ("sq", s_)], [vk])
            p.act(lambda e, vb=vb: e.activation(out=sd[:], in_=vb[:], func=AF.Sqrt, bias=st.epsb[:], scale=1.0),
                  r=[vk, "epsb"], w=[("sd", s_)])
            p.dve(lambda e: e.reciprocal(out=sd[:], in_=sd[:]), r=[("sd", s_)], w=[("sd", s_)])
            p.pool(lambda e: e.tensor_tensor(out=cen[:], in0=cen[:], in1=sd[:], op=ALU.mult),
                   r=[("cen", s_), ("sd", s_)], w=[("cen", s_)])
            p.dve(lambda e, h=h: e.scalar_tensor_tensor(out=yT[4 + h][:], in0=cen[:], scalar=retg[:, h:h + 1],
                                                        in1=sgT[:, h, :], op0=ALU.mult, op1=ALU.mult),
                  r=[("cen", s_), "retg", ("sgT", h)], w=[("yT", 4 + h)])
        for pair in range(2):
            ops_a = capture_ops(p, ln_head, 2 * pair)
            ops_b = capture_ops(p, ln_head, 2 * pair + 1)
            kz = 0
            while kz < max(len(ops_a), len(ops_b)):
                p.ops.extend(ops_a[kz:kz + 2])
                p.ops.extend(ops_b[kz:kz + 2])
                kz += 2
        for g in range(4):
            pp, pk = st.bank(R4)
            for i in range(4):
                n = B * 4 + i
                ucur = u_tm[ub][:, i, g * 128:(g + 1) * 128]
                if n == 0:
                    mm(p, pp[:, 0:128], ucur, pm[:, g, 0, :], True, True, [("u_tm", ub, i), "pm"], [pk])
                else:
                    if i > 0:
                        uprev, pkey = u_tm[ub][:, i - 1, g * 128:(g + 1) * 128], ("u_tm", ub, i - 1)
                    else:
                        uprev, pkey = u_tm[1 - ub][:, 3, g * 128:(g + 1) * 128], ("u_tm", 1 - ub, 3)
                    mm(p, pp[:, i * 128:(i + 1) * 128], ucur, pm[:, g, 1, :], True, False,
                       [("u_tm", ub, i), "pm"], [pk])
                    mm(p, pp[:, i * 128:(i + 1) * 128], uprev, pm[:, g, 2, :], False, True, [pkey, "pm"], [pk])
            p.act(lambda e, pp=pp: e.activation(out=pT_sb[:], in_=pp[:], func=AF.Copy), r=[pk], w=["pT_sb"])
            ya, yk = st.bank(R4)
            mm(p, ya[:], poolw[:, g, :], pT_sb[:], True, True, ["poolw", "pT_sb"], [yk])
            p.dve(lambda e, ya=ya, g=g: e.tensor_scalar(out=yT[g][:], in0=ya[:], scalar1=pscale[:, g:g + 1],
                                                        scalar2=None, op0=ALU.mult),
                  r=[yk, "pscale"], w=[("yT", g)])
        for i in range(4):
            mix = []
            mkeys = []
            for hf in range(2):
                mb, mk = st.bank(R4)
                for f in range(8):
                    mm(p, mb[:], yT[f][:, i * 128:(i + 1) * 128], wout[:, f, hf * 512:(hf + 1) * 512],
                       f == 0, f == 7, [("yT", f), "wout"], [mk])
                mix.append(mb[:])
                mkeys.append(mk)
            ln_epilogue(st, mix, mkeys, xt[:, i, :], [("ab", "xt", i)],
                        dst[B * 512 + i * 128:B * 512 + (i + 1) * 128, :], ("ab", B, i))
    p.flush()


def stage_c(p, dram, src, dst, S):
    st = Stage(p, dram)
    alloc_ln(st)
    st.load_ln(2)
    nblk = S // 512
    R4 = [0, 1, 2, 3]
    RS = [4, 5]
    RK = [6, 7]
    wfm = p.sb([128, 8, 3072], BF16)
    wtm = p.sb([128, 8, 1024], BF16)
    wout = p.sb([128, 8, 1024], BF16)
    wload(p, wfm, dram["c_wfm"], "wfm")
    wload(p, wtm, dram["c_wtm"], "wtm")
    wload(p, wout, dram["c_wout"], "wout")
    lbraw = p.sb([128, 2, 8], F32)
    lb = p.sb([128, 8], F32)
    oml = p.sb([128, 8], F32)
    cg = p.sb([128, 8], F32)
    maskC = p.sb([128, 128], F32)
    rmask = p.sb([128, 512], F32)
    onesd = p.sb([128, 128], F32)
    p.dma("sp", lbraw[:], dram["c_lb"], w=["lbraw"])
    p.dma("sp", cg[:], dram["c_g"], w=["cg"])
    p.dma("sp", maskC[:], dram["c_maskC"], w=["maskC"])
    p.dve(lambda e: e.memset(onesd[:], 1.0 / 128.0), w=["onesd"])
    p.dve(lambda e: e.memset(rmask[:], 1.0), w=["rmask"])
    p.dve(lambda e: e.memset(rmask[:].rearrange("p (c j) -> p c j", j=32)[:, :, 0:1], 0.0), w=["rmask"])
    p.dve(lambda e: e.tensor_tensor(out=lb[:], in0=lbraw[:, 1, :], in1=lbraw[:, 0, :], op=ALU.subtract),
          r=["lbraw"], w=["lb"])
    p.act(lambda e: e.activation(out=lb[:], in_=lb[:], func=AF.Sigmoid), r=["lb"], w=["lb"])
    p.dve(lambda e: e.tensor_scalar(out=oml[:], in0=lb[:], scalar1=-1.0, scalar2=1.0, op0=ALU.mult, op1=ALU.add),
          r=["lb"], w=["oml"])

    xt = p.sb([128, 4, 1024], F32)
    xb = p.sb([128, 4, 1024], BF16)
    hT = p.sb([128, 8, 512], BF16)
    v_tm = p.sb([128, 4, 1024], BF16)
    TS = [[p.sb([128, 512], F32) for _ in range(6)] + [p.sb([128, 512], BF16)] for _ in range(2)]
    dcy = [p.sb([128, 16], F32) for _ in range(4)]
    q_in = [p.sb([128, 512], BF16) for _ in range(4)]
    k_in = [p.sb([128, 512], BF16) for _ in range(4)]
    kotm = [p.sb([128, 4, 128], BF16) for _ in range(4)]
    kotm3 = [p.sb([128, 4, 128], BF16) for _ in range(4)]
    m96 = p.sb([128, 1], F32)
    p.dma("sp", m96[:], dram["c_m96"], w=["m96"])
    sgT = [p.sb([128, 512], F32) for _ in range(4)]
    scTm = [p.sb([128, 128], BF16) for _ in range(4)]
    kvrr = [0]
    state = [p.sb([128, 128], F32) for _ in range(8)]
    stbf = [p.sb([128, 128], BF16) for _ in range(8)]
    o_sb = p.sb([128, 512], F32)
    sq = p.sb([128, 512], F32)
    sd = p.sb([128, 512], F32)
    yT = [p.sb([128, 512], BF16) for _ in range(8)]
    for h in range(8):
        p.dve(lambda e, h=h: e.memset(state[h][:], 0.0), w=[("state", h)])
        p.dve(lambda e, h=h: e.memset(stbf[h][:], 0.0), w=[("stbf", h)])

    if DBG == 99:
        print("SBUF free stage_c", p.nc.sbuf_bytes_remaining)
    hTk = [("c", "hT", i) for i in range(4)]
    for B in range(nblk):
        load_transpose(st, src[B * 512:(B + 1) * 512, :], 4, xt, xb, hT, "c", R4)
        for i in range(4):
            for hf in range(2):
                bk, kk = st.bank(R4)
                for k in range(8):
                    mm(p, bk[:], hT[:, k, i * 128:(i + 1) * 128], wtm[:, k, hf * 512:(hf + 1) * 512],
                       k == 0, k == 7, ["wtm"] + hTk, [kk])
                p.act(lambda e, bk=bk, i=i, hf=hf: e.activation(out=v_tm[:, i, hf * 512:(hf + 1) * 512], in_=bk[:],
                                                                func=AF.Copy), r=[kk], w=[("v_tm", i, hf)])
        if DBG == 1:
            continue
        for hp in range(2):
            def prep_head(hl):
                h = hp * 4 + hl
                s_ = hl % 2
                f_sb, lf, bcs, tmp, kk_sb, eb, k_out = TS[s_]
                RB = [0, 1] if s_ == 0 else [2, 3]
                bf_, kf = st.bank(RB)
                for k in range(8):
                    mm(p, bf_[:], wfm[:, k, 1024 + h * 128:1024 + (h + 1) * 128], hT[:, k, :], k == 0, k == 7,
                       ["wfm"] + hTk, [kf])
                p.act(lambda e, bf_=bf_: e.activation(out=f_sb[:], in_=bf_[:], func=AF.Sigmoid), r=[kf], w=[("f_sb", s_)])
                p.dve(lambda e, h=h: e.tensor_scalar(out=f_sb[:], in0=f_sb[:], scalar1=oml[:, h:h + 1],
                                                     scalar2=lb[:, h:h + 1], op0=ALU.mult, op1=ALU.add),
                      r=[("f_sb", s_), "oml", "lb"], w=[("f_sb", s_)])
                p.act(lambda e: e.activation(out=lf[:], in_=f_sb[:], func=AF.Ln), r=[("f_sb", s_)], w=[("lf", s_)])
                p.pool(lambda e: e.tensor_scalar(out=kk_sb[:], in0=f_sb[:], scalar1=-1.0, scalar2=1.0,
                                                 op0=ALU.mult, op1=ALU.add), r=[("f_sb", s_)], w=[("kk_sb", s_)])
                p.dve(lambda e: e.tensor_tensor_scan(out=bcs[:], data0=rmask[:], data1=lf[:], initial=0.0,
                                                     op0=ALU.mult, op1=ALU.add), r=["rmask", ("lf", s_)], w=[("bcs", s_)])
                p.act(lambda e: e.activation(out=eb[:], in_=bcs[:], func=AF.Exp), r=[("bcs", s_)], w=[("eb", s_)])
                p.dve(lambda e, hl=hl: e.tensor_copy(out=dcy[hl][:],
                                                     in_=eb[:].rearrange("p (c j) -> p c j", j=32)[:, :, 31]),
                      r=[("eb", s_)], w=[("dcy", hl)])
                p.act(lambda e: e.activation(out=tmp[:], in_=bcs[:], func=AF.Exp, scale=-1.0), r=[("bcs", s_)], w=[("tmp", s_)])
                p.pool(lambda e, hl=hl: e.tensor_tensor(out=k_in[hl][:], in0=kk_sb[:], in1=tmp[:], op=ALU.mult),
                       r=[("kk_sb", s_), ("tmp", s_)], w=[("k_in", hl)])
                b3 = bcs[:].rearrange("p (c j) -> p c j", j=32)
                p.pool(lambda e, b3=b3: e.tensor_tensor(out=tmp[:].rearrange("p (c j) -> p c j", j=32),
                                                        in0=b3[:, :, 31:32].broadcast_to([128, 16, 32]), in1=b3,
                                                        op=ALU.subtract), r=[("bcs", s_), ("k_in", hl)], w=[("tmp", s_)])
                p.act(lambda e: e.activation(out=tmp[:], in_=tmp[:], func=AF.Exp), r=[("tmp", s_)], w=[("tmp", s_)])
                p.pool(lambda e: e.tensor_tensor(out=k_out[:], in0=kk_sb[:], in1=tmp[:], op=ALU.mult),
                       r=[("kk_sb", s_), ("tmp", s_)], w=[("k_out", s_)])
                bq, kq = st.bank(RB)
                for k in range(8):
                    mm(p, bq[:], wfm[:, k, h * 128:(h + 1) * 128], hT[:, k, :], k == 0, k == 7, ["wfm"] + hTk, [kq])
                p.dve(lambda e, bq=bq, hl=hl: e.tensor_tensor(out=q_in[hl][:], in0=bq[:], in1=eb[:], op=ALU.mult),
                      r=[kq, ("eb", s_)], w=[("q_in", hl)])
                bg, kg = st.bank(RB)
                for k in range(8):
                    mm(p, bg[:], wfm[:, k, 2048 + h * 128:2048 + (h + 1) * 128], hT[:, k, :], k == 0, k == 7,
                       ["wfm"] + hTk, [kg])
                p.act(lambda e, bg=bg, hl=hl: e.activation(out=sgT[hl][:], in_=bg[:], func=AF.Silu),
                      r=[kg], w=[("sgT", hl)])
                bt, kt = st.bank(RB)
                btv = bt[:].bitcast(BF16)[:, 0:512].rearrange("p (i s) -> p i s", s=128)
                for i in range(4):
                    p.pe(lambda e, i=i, btv=btv: e.transpose(out=btv[:, i, :], in_=k_out[:, i * 128:(i + 1) * 128],
                                                             identity=st.ident[:]), r=[("k_out", s_), "ident"], w=[kt])
                p.act(lambda e, btv=btv, hl=hl: e.activation(out=kotm[hl][:], in_=btv, func=AF.Copy),
                      r=[kt], w=[("kotm", hl)])
                p.dve(lambda e, hl=hl: e.tensor_scalar(out=kotm3[hl][:], in0=kotm[hl][:], scalar1=m96[:, 0:1],
                                                       scalar2=None, op0=ALU.mult),
                      r=[("kotm", hl), "m96"], w=[("kotm3", hl)])
            for pair in range(2):
                ops_a = capture_ops(p, prep_head, 2 * pair)
                ops_b = capture_ops(p, prep_head, 2 * pair + 1)
                kz = 0
                while kz < max(len(ops_a), len(ops_b)):
                    p.ops.extend(ops_a[kz:kz + 2])
                    p.ops.extend(ops_b[kz:kz + 2])
                    kz += 2
            if DBG in (2, 21, 22, 23, 24):
                continue
            for i in range(4):
                sms = []
                for hl in range(4):
                    h = hp * 4 + hl
                    sc, ks = st.bank(RS)
                    sm = scTm[hl]
                    smk = ("scTm", hl)
                    mm(p, sc[:, 0:128], k_in[hl][:, i * 128:(i + 1) * 128], q_in[hl][:, i * 128:(i + 1) * 128],
                       True, True, [("k_in", hl), ("q_in", hl)], [ks])
                    p.dve(lambda e, sc=sc, sm=sm: e.tensor_tensor(out=sm[:], in0=sc[:, 0:128], in1=maskC[:],
                                                                  op=ALU.mult), r=[ks, "maskC"], w=[smk])
                for c4 in range(4):
                    c0 = i * 128 + c4 * 32
                    cc = i * 4 + c4
                    kvs = []
                    for hl in range(4):
                        h = hp * 4 + hl
                        vkey = ("v_tm", i, h // 4)
                        slot = kvrr[0] % 8
                        kvrr[0] += 1
                        kvb = st.PS[RK[slot // 4]]
                        kv = kvb[:, (slot % 4) * 128:(slot % 4 + 1) * 128]
                        kk2 = ("ps", RK[slot // 4])
                        if c4 < 3:
                            mm(p, kv, kotm[hl][c4 * 32:(c4 + 1) * 32, i, :],
                               v_tm[c4 * 32:(c4 + 1) * 32, i, h * 128:(h + 1) * 128], True, True,
                               [("kotm", hl), vkey], [kk2])
                        else:
                            mm(p, kv, kotm3[hl][64:128, i, :],
                               v_tm[64:128, i, h * 128:(h + 1) * 128], True, True,
                               [("kotm3", hl), vkey], [kk2])
                        kvs.append((kv, kk2))
                    for hl in range(4):
                        h = hp * 4 + hl
                        ob, ok = st.PS[hl], ("ps", hl)
                        vsl = v_tm[:, i, h * 128:(h + 1) * 128]
                        vkey = ("v_tm", i, h // 4)
                        sm = scTm[hl]
                        smk = ("scTm", hl)
                        mm(p, ob[:, c0:c0 + 32], vsl, sm[:, c4 * 32:(c4 + 1) * 32], True, False, [vkey, smk], [ok])
                        mm(p, ob[:, c0:c0 + 32], stbf[h][:], q_in[hl][:, c0:c0 + 32], False, True,
                           [("stbf", h), ("q_in", hl)], [ok])
                    for hl in range(4):
                        h = hp * 4 + hl
                        kv, kk2 = kvs[hl]
                        p.dve(lambda e, kv=kv, h=h, hl=hl, cc=cc: e.scalar_tensor_tensor(
                            out=state[h][:], in0=state[h][:], scalar=dcy[hl][:, cc:cc + 1], in1=kv,
                            op0=ALU.mult, op1=ALU.add), r=[kk2, ("state", h), ("dcy", hl)], w=[("state", h)])
                        p.act(lambda e, h=h: e.activation(out=stbf[h][:], in_=state[h][:], func=AF.Copy),
                              r=[("state", h)], w=[("stbf", h)])
            if DBG == 3:
                continue
            for hl in range(4):
                h = hp * 4 + hl
                ob, ok = st.PS[hl], ("ps", hl)
                p.act(lambda e, ob=ob: e.activation(out=o_sb[:], in_=ob[:], func=AF.Copy), r=[ok], w=["o_sb"])
                p.pool(lambda e: e.tensor_tensor(out=sq[:], in0=o_sb[:], in1=o_sb[:], op=ALU.mult),
                       r=["o_sb"], w=["sq"])
                vb, vk = st.bank(RS)
                mm(p, vb[:], onesd[:], sq[:], True, True, ["onesd", "sq"], [vk])
                p.act(lambda e, vb=vb: e.activation(out=sd[:], in_=vb[:], func=AF.Sqrt, bias=st.epsb[:], scale=1.0),
                      r=[vk, "epsb"], w=["sd"])
                p.dve(lambda e: e.reciprocal(out=sd[:], in_=sd[:]), r=["sd"], w=["sd"])
                p.pool(lambda e: e.tensor_tensor(out=o_sb[:], in0=o_sb[:], in1=sd[:], op=ALU.mult),
                       r=["o_sb", "sd"], w=["o_sb"])
                p.dve(lambda e, h=h, hl=hl: e.scalar_tensor_tensor(out=yT[h][:], in0=o_sb[:], scalar=cg[:, h:h + 1],
                                                                   in1=sgT[hl][:], op0=ALU.mult, op1=ALU.mult),
                      r=["o_sb", "cg", ("sgT", hl)], w=[("yT", h)])
        if DBG in (2, 3, 21, 22, 23, 24):
            continue
        for i in range(4):
            mix = []
            mkeys = []
            for hf in range(2):
                mb, mk = st.bank(R4)
                for f in range(8):
                    mm(p, mb[:], yT[f][:, i * 128:(i + 1) * 128], wout[:, f, hf * 512:(hf + 1) * 512],
                       f == 0, f == 7, [("yT", f), "wout"], [mk])
                mix.append(mb[:])
                mkeys.append(mk)
            ln_epilogue(st, mix, mkeys, xt[:, i, :], [("c", "xt", i)],
                        dst[B * 512 + i * 128:B * 512 + (i + 1) * 128, :], ("c", B, i))
    p.flush()


def stage_peer(p, dram, src, dst, S, l):
    st = Stage(p, dram, banks=False)
    alloc_ln(st)
    st.load_ln(2 * l + 1)
    ngrp = S // 256
    PSall = p.ps([128, 4096], F32)

    def bk(b, n=1):
        return PSall[:, b * 512:(b + n) * 512]

    st.PS = [bk(b) for b in range(8)]
    WQ = dram[f"p{l}_wq"].rearrange("(k p) n -> p k n", p=128)
    keysT = p.sb([128, 16, 128], BF16)
    p.dma("pool", keysT[:], dram[f"p{l}_keysT"], w=["keysT"])
    bmask = p.sb([128, 8, 16], BF16)
    iota = p.sb([128, 128, 8], BF16)
    p.dma("sp", bmask[:], dram["p_bmask"].rearrange("p (h a) -> p h a", a=16), w=["bmask"])
    p.dma("sp", iota[:], dram["p_iota"], w=["iota"])
    UT = dram[f"p{l}_ut"]
    VV = dram[f"p{l}_v"]
    UT16 = dram[f"p{l}_ut16"]
    V16 = dram[f"p{l}_v16"]

    xt = [p.sb([128, 2, 1024], F32) for _ in range(2)]
    xb = p.sb([128, 1, 1024], BF16)
    hT = [p.sb([128, 8, 256], BF16) for _ in range(2)]
    wqb = [p.sb([128, 8, 128], BF16) for _ in range(3)]
    qT_sb = p.sb([128, 16, 256], BF16)
    s_sb = p.sb([128, 16, 128], F32)
    s2 = p.sb([128, 16, 128], F32)
    v16 = p.sb([128, 16, 16], F32)
    i16 = p.sb([128, 16, 16], U32)
    idxf = p.sb([128, 2, 8, 16], BF16)
    cand = p.sb([128, 8, 256], F32)
    cand2 = s2[:].rearrange("p (h two) k -> p h (two k)", two=2)
    m8 = p.sb([128, 8, 16], F32)
    ex = p.sb([128, 8, 256], F32)
    msk = s_sb[:].rearrange("p (h two) k -> p h (two k)", two=2)
    zsum = p.sb([128, 8], F32)
    Gc = p.sb([128, 16, 8, 16], BF16)
    idxT = [p.sb([128, 2, 128], BF16) for _ in range(2)]
    GcT = [p.sb([128, 128, 16], BF16) for _ in range(2)]
    P1 = [p.sb([128, 128, 8], BF16) for _ in range(2)]
    P2 = [p.sb([128, 128, 8], BF16) for _ in range(2)]
    BDT = [p.sb([128, 8, 128], BF16) for _ in range(2)]
    M_sb = [p.sb([128, 8, 128], BF16) for _ in range(2)]
    GT = p.sb([128, 128, 256], BF16)
    NSL = 4
    utb = [p.sb([128, 8, 128], BF16) for _ in range(NSL)]
    vb = [p.sb([128, 1024], BF16) for _ in range(NSL)]
    ge = [p.sb([128, 256], F32) for _ in range(2)]
    cd = [p.sb([128, 256], BF16) for _ in range(2)]
    R67 = [6, 7]
    if DBG == 99:
        print("SBUF free stage_peer", p.nc.sbuf_bytes_remaining)

    def emit_A1(G):
        par = G % 2
        tag = ("pe", par)
        hTk = [(tag, "hT", i) for i in range(2)]
        load_transpose(st, src[G * 256:(G + 1) * 256, :], 2, xt[par], xb, hT[par], tag, R67)
        for hp in range(16):
            ws = hp % 3
            p.dma("pool", wqb[ws][:], WQ[:, :, hp * 128:(hp + 1) * 128], w=[("wqb", ws)])
            qb, qk = st.bank(R67)
            for k in range(8):
                mm(p, qb[:, 0:256], wqb[ws][:, k, :], hT[par][:, k, :], k == 0, k == 7,
                   [("wqb", ws)] + hTk, [qk])
            p.act(lambda e, qb=qb, hp=hp: e.activation(out=qT_sb[:, hp, :], in_=qb[:, 0:256], func=AF.Copy),
                  r=[qk], w=[("qT", hp)])
        for i in range(2):
            for b4 in range(4):
                sb_, sk_ = st.bank(R67)
                for q4 in range(4):
                    hp = b4 * 4 + q4
                    mm(p, sb_[:, q4 * 128:(q4 + 1) * 128], qT_sb[:, hp, i * 128:(i + 1) * 128],
                       keysT[:, hp, :], True, True, [("qT", hp), "keysT"], [sk_])
                p.act(lambda e, b4=b4, sb_=sb_: e.activation(
                    out=s_sb[:, 4 * b4:4 * b4 + 4, :].rearrange("p a k -> p (a k)"), in_=sb_, func=AF.Copy),
                    r=[sk_], w=[("s_sb", b4)])
            for g in range(16):
                sk = ("s_sb", g // 4)
                p.dve(lambda e, g=g: e.max(out=v16[:, g, 0:8], in_=s_sb[:, g, :]), r=[sk], w=[("v16", g)])
                p.dve(lambda e, g=g: e.max_index(out=i16[:, g, 0:8], in_max=v16[:, g, 0:8], in_values=s_sb[:, g, :]),
                      r=[sk, ("v16", g)], w=[("i16", g)])
                p.dve(lambda e, g=g: e.match_replace(out=s2[:, g, :], in_to_replace=v16[:, g, 0:8],
                                                     in_values=s_sb[:, g, :], imm_value=NEG),
                      r=[sk, ("v16", g)], w=[("s2", g)])
                p.dve(lambda e, g=g: e.max(out=v16[:, g, 8:16], in_=s2[:, g, :]), r=[("s2", g)], w=[("v16", g)])
                p.dve(lambda e, g=g: e.max_index(out=i16[:, g, 8:16], in_max=v16[:, g, 8:16], in_values=s2[:, g, :]),
                      r=[("s2", g), ("v16", g)], w=[("i16", g)])
            vk = [("v16", g) for g in range(16)]
            ik = [("i16", g) for g in range(16)]
            p.dve(lambda e: e.tensor_copy(out=idxf[:].rearrange("p two h a -> p h two a"),
                                          in_=i16[:].rearrange("p (h two) a -> p h two a", two=2)),
                  r=ik, w=["idxf"])
            v4 = v16[:].rearrange("p (h two) a -> p h two a", two=2)
            c4 = cand[:].rearrange("p h (a b) -> p h a b", b=16)
            p.dve(lambda e, v4=v4, c4=c4: e.tensor_tensor(
                out=c4, in0=v4[:, :, 0, :].unsqueeze(3).broadcast_to([128, 8, 16, 16]),
                in1=v4[:, :, 1, :].unsqueeze(2).broadcast_to([128, 8, 16, 16]), op=ALU.add), r=vk, w=["cand"])
            for h in range(8):
                p.dve(lambda e, h=h: e.max(out=m8[:, h, 0:8], in_=cand[:, h, :]), r=["cand"], w=[("m8", h)])
                p.dve(lambda e, h=h: e.match_replace(out=cand2[:, h, :], in_to_replace=m8[:, h, 0:8],
                                                     in_values=cand[:, h, :], imm_value=NEG),
                      r=["cand", ("m8", h)], w=[("cand2", h), ("s2", 2 * h), ("s2", 2 * h + 1)])
                p.dve(lambda e, h=h: e.max(out=m8[:, h, 8:16], in_=cand2[:, h, :]), r=[("cand2", h)], w=[("m8", h)])
            mk = [("m8", h) for h in range(8)]
            p.dve(lambda e: e.tensor_tensor(out=ex[:], in0=cand[:], in1=m8[:, :, 0:1].broadcast_to([128, 8, 256]),
                                            op=ALU.subtract), r=["cand"] + mk, w=["ex"])
            p.act(lambda e: e.activation(out=ex[:], in_=ex[:], func=AF.Exp), r=["ex"], w=["ex"])
            p.dve(lambda e: e.tensor_tensor(out=msk, in0=cand[:], in1=m8[:, :, 15:16].broadcast_to([128, 8, 256]),
                                            op=ALU.is_ge), r=["cand"] + mk,
                  w=["msk"] + [("s_sb", b) for b in range(4)])
            p.pool(lambda e: e.tensor_tensor(out=ex[:], in0=ex[:], in1=msk, op=ALU.mult), r=["ex", "msk"], w=["ex"])
            p.dve(lambda e: e.tensor_reduce(out=zsum[:], in_=ex[:], axis=AX.X, op=ALU.add), r=["ex"], w=["zsum"])
            p.dve(lambda e: e.reciprocal(out=zsum[:], in_=zsum[:]), r=["zsum"], w=["zsum"])
            p.dve(lambda e: e.tensor_tensor(
                out=Gc[:].rearrange("p a h b -> p h a b"), in0=ex[:].rearrange("p h (a b) -> p h a b", b=16),
                in1=zsum[:].unsqueeze(2).unsqueeze(3).broadcast_to([128, 8, 16, 16]), op=ALU.mult),
                r=["ex", "zsum"], w=["Gc"])
            tbk, tkey = st.bank(R67)
            tb = tbk.bitcast(BF16)
            for two in range(2):
                p.pe(lambda e, two=two, tb=tb: e.transpose(
                    out=tb[:, two * 128:(two + 1) * 128], in_=idxf[:, two, :, :].rearrange("p h a -> p (h a)"),
                    identity=st.ident[:]),
                    r=["idxf", "ident"], w=[tkey])
            p.act(lambda e, tb=tb, i=i: e.activation(out=idxT[i][:].rearrange("p a t -> p (a t)"), in_=tb[:, 0:256],
                                                     func=AF.Copy), r=[tkey], w=[("idxT", i)])
            for half in range(2):
                gbk, gkey = st.bank(R67)
                gb = gbk.bitcast(BF16).rearrange("p (a t) -> p a t", t=128)
                for a8 in range(8):
                    a = half * 8 + a8
                    p.pe(lambda e, a=a, a8=a8, gb=gb: e.transpose(
                        out=gb[:, a8, :], in_=Gc[:, a, :, :].rearrange("p h b -> p (h b)"), identity=st.ident[:]),
                        r=["Gc", "ident"], w=[gkey])
                p.act(lambda e, gb=gb, half=half, i=i: e.activation(
                    out=GcT[i][:].rearrange("p t a -> p a t")[:, half * 8:(half + 1) * 8, :], in_=gb, func=AF.Copy),
                    r=[gkey], w=[("GcT", i, half)])

    def emit_A2(G):
        steps = [(i, sbk) for i in range(2) for sbk in range(16)]

        def banks(n):
            pb = n % 2
            mb0 = 4 if pb == 0 else 0
            gb0 = 6 if pb == 0 else 2
            return pb, mb0, gb0

        def part_m(n):
            i, sbk = steps[n]
            t0 = sbk * 8
            pb, mb0, gb0 = banks(n)
            mkeys2 = [("ps", mb0), ("ps", mb0 + 1)]
            P2_, BDT_, M_ = P2[pb], BDT[pb], M_sb[pb]
            p.dve(lambda e: e.tensor_tensor(
                out=P2_[:], in0=iota[:],
                in1=idxT[i][:, 1, t0:t0 + 8].unsqueeze(1).broadcast_to([128, 128, 8]), op=ALU.is_equal),
                r=["iota", ("idxT", i)], w=[("P2", pb)])
            p.dve(lambda e: e.tensor_tensor(
                out=BDT_[:].rearrange("p t (h a) -> p t h a", a=16),
                in0=GcT[i][:, t0:t0 + 8, :].unsqueeze(2).broadcast_to([128, 8, 8, 16]),
                in1=bmask[:].unsqueeze(1).broadcast_to([128, 8, 8, 16]), op=ALU.mult),
                r=[("GcT", i, 0), ("GcT", i, 1), "bmask"], w=[("BDT", pb)])
            for t in range(8):
                mm(p, bk(mb0, 2)[:, t * 128:(t + 1) * 128], BDT_[:, t, :], P2_[:, :, t], True, True,
                   [("BDT", pb), ("P2", pb)], mkeys2)
            p.act(lambda e: e.activation(out=M_[:].rearrange("p t j -> p (t j)"), in_=bk(mb0, 2), func=AF.Copy),
                  r=mkeys2, w=[("M_sb", pb)])

        def part_g(n):
            i, sbk = steps[n]
            t0 = sbk * 8
            pb, mb0, gb0 = banks(n)
            gkeys2 = [("ps", gb0), ("ps", gb0 + 1)]
            P1_, M_ = P1[pb], M_sb[pb]
            p.dve(lambda e: e.tensor_tensor(
                out=P1_[:], in0=iota[:],
                in1=idxT[i][:, 0, t0:t0 + 8].unsqueeze(1).broadcast_to([128, 128, 8]), op=ALU.is_equal),
                r=["iota", ("idxT", i)], w=[("P1", pb)])
            for t in range(8):
                mm(p, bk(gb0, 2)[:, t * 128:(t + 1) * 128], P1_[:, :, t], M_[:, t, :], True, True,
                   [("P1", pb), ("M_sb", pb)], gkeys2)
            gsrc = bk(gb0, 2).rearrange("p (t j) -> p j t", j=128)
            gdst = GT[:, :, i * 128 + t0:i * 128 + t0 + 8]
            if n % 3 == 2:
                p.dve(lambda e: e.tensor_copy(out=gdst, in_=gsrc), r=gkeys2, w=[("GT", i)])
            else:
                p.act(lambda e: e.activation(out=gdst, in_=gsrc, func=AF.Copy), r=gkeys2, w=[("GT", i)])

        part_m(0)
        for n in range(len(steps)):
            if n + 1 < len(steps):
                part_m(n + 1)
            part_g(n)

    def emit_B(G, extra):
        par = G % 2
        tag = ("pe", par)
        hTk = [(tag, "hT", i) for i in range(2)]
        nex = len(extra)
        done = 0
        def v_part(j):
            sl = j % NSL
            g2 = j % 2
            for i in range(2):
                for hf in range(2):
                    b = 2 * i + hf
                    mm(p, bk(b), cd[g2][:, i * 128:(i + 1) * 128], vb[sl][:, hf * 512:(hf + 1) * 512],
                       j == 0, j == 127, [("cd", g2), ("vb", sl)], [("ps", b)])

        for j in range(128):
            sl = j % NSL
            if G == 0:
                p.dma("pool", utb[sl][:], UT[j], w=[("utb", sl)])
                p.dma("pool", vb[sl][:], VV[j], w=[("vb", sl)])
                p.dma("sp", UT16[j], utb[sl][:], r=[("utb", sl)], w=[("ut16", j)])
                p.dma("sp", V16[j], vb[sl][:], r=[("vb", sl)], w=[("v16d", j)])
            else:
                p.dma("sp", utb[sl][:], UT16[j], r=[("ut16", j)], w=[("utb", sl)])
                p.dma("act", vb[sl][:], V16[j], r=[("v16d", j)], w=[("vb", sl)])
            ab = 4 + j % 2
            for k in range(8):
                mm(p, bk(ab)[:, 0:256], utb[sl][:, k, :], hT[par][:, k, :], k == 0, k == 7,
                   [("utb", sl)] + hTk, [("ps", ab)])
            g2 = j % 2
            p.act(lambda e, ab=ab, g2=g2: e.activation(out=ge[g2][:], in_=bk(ab)[:, 0:256], func=AF.Gelu),
                  r=[("ps", ab)], w=[("ge", g2)])
            p.dve(lambda e, g2=g2, j=j: e.tensor_tensor(out=cd[g2][:], in0=ge[g2][:], in1=GT[:, j, :], op=ALU.mult),
                  r=[("ge", g2), ("GT", 0), ("GT", 1)], w=[("cd", g2)])
            if j > 0:
                v_part(j - 1)
            upto = (nex * (j + 1)) // 120 if j < 119 else nex
            upto = min(upto, nex)
            if upto > done:
                p.ops.extend(extra[done:upto])
                done = upto
        v_part(127)
        for i in range(2):
            ln_epilogue(st, [bk(2 * i), bk(2 * i + 1)], [("ps", 2 * i), ("ps", 2 * i + 1)], xt[par][:, i, :],
                        [(tag, "xt", i)], dst[G * 256 + i * 128:G * 256 + (i + 1) * 128, :], ("pe", G, i))

    def capture(fn, *a):
        saved = p.ops
        p.ops = []
        fn(*a)
        got = p.ops
        p.ops = saved
        return got

    emit_A1(0)
    for G in range(ngrp):
        emit_A2(G)
        extra = capture(emit_A1, G + 1) if G + 1 < ngrp else []
        emit_B(G, extra)
    p.flush()


def _bf(a):
    return np.ascontiguousarray(a).astype(ml_dtypes.bfloat16)


def _consts(S):
    c = {}
    c["ident"] = _bf(np.eye(128, dtype=np.float32))
    half = 32
    inv = 1.0 / (10000.0 ** np.linspace(0.0, 1.0, half, dtype=np.float32)).astype(np.float32)
    pos = np.arange(S, dtype=np.float32)
    ang = (pos[None, :] * inv[:, None]).astype(np.float32)
    cos = np.cos(ang).astype(np.float32)
    sin = np.sin(ang).astype(np.float32)
    c["ab_cos"] = np.ascontiguousarray(np.concatenate([cos, cos], 0))
    c["ab_sin"] = np.ascontiguousarray(np.concatenate([-sin, sin], 0))
    idx = np.arange(128, dtype=np.float64)
    dq = np.zeros((64, 4, 128), np.float32)
    dk = np.zeros((64, 4, 128), np.float32)
    for h in range(4):
        lg = math.log1p(-2.0 ** (-5.0 - h))
        dq[:, h, :] = np.exp(lg * (idx + 1.0))[None, :]
        dk[:, h, :] = (np.exp(-lg * (idx + 1.0)) * (64 ** -0.5))[None, :]
    c["ab_dq"] = dq
    c["ab_dk"] = dk
    s_ = np.arange(128)[:, None]
    t_ = np.arange(128)[None, :]
    c["ab_maskT"] = (s_ <= t_).astype(np.float32)
    pm = np.zeros((128, 4, 3, 128), np.float32)
    for g, w in enumerate((2, 4, 8, 16)):
        cnt = np.minimum(t_ + 1, w).astype(np.float32)
        inwin = (s_ <= t_) & (s_ > t_ - w)
        pm[:, g, 0, :] = inwin / cnt - (s_ == t_)
        pm[:, g, 1, :] = inwin / float(w) - (s_ == t_)
        pm[:, g, 2, :] = ((s_ - 128) > (t_ - w)) / float(w)
    c["ab_pm"] = _bf(pm)
    c["c_m96"] = (np.arange(128) >= 96).astype(np.float32).reshape(128, 1)
    c["c_maskC"] = ((s_ <= t_) & ((s_ // 32) == (t_ // 32))).astype(np.float32)
    hh = np.arange(128) // 16
    c["p_bmask"] = _bf((hh[:, None] == hh[None, :]).astype(np.float32))
    c["p_iota"] = _bf(np.broadcast_to(np.arange(128, dtype=np.float32)[None, :, None], (128, 128, 8)))
    return c


def _prep_weights(inp):
    w = {}
    ab = np.asarray(inp["ab_w_in"])[0]
    cols = []
    swp = np.concatenate([np.arange(32, 64), np.arange(0, 32)])
    for h in range(4):
        q0 = 512 + h * 64
        k0 = 768 + h * 64
        cols += [np.arange(q0, q0 + 64), q0 + swp, np.arange(k0, k0 + 64), k0 + swp]
    cols.append(np.arange(1536, 2048))
    w["ab_wfm"] = np.ascontiguousarray(ab[:, np.concatenate(cols)])
    w["ab_wtm"] = np.ascontiguousarray(ab[:, np.concatenate([np.arange(0, 512), np.arange(1024, 1536)])])
    w["ab_wout"] = np.ascontiguousarray(np.asarray(inp["ab_w_out"])[0])
    w["ab_poolw"] = np.ascontiguousarray(np.asarray(inp["pool_w"])[0].transpose(1, 0, 2))
    w["ab_pscale"] = np.ascontiguousarray(np.asarray(inp["pool_scale"])[0].reshape(4, 128).T)
    w["ab_retg"] = np.ascontiguousarray(np.asarray(inp["ret_norm_g"])[0].reshape(4, 128).T)
    cw = np.asarray(inp["c_w_in"])[0]
    w["c_wfm"] = np.ascontiguousarray(cw[:, np.concatenate([np.arange(0, 2048), np.arange(3072, 4096)])])
    w["c_wtm"] = np.ascontiguousarray(cw[:, 2048:3072])
    w["c_wout"] = np.ascontiguousarray(np.asarray(inp["c_w_out"])[0])
    w["c_lb"] = np.ascontiguousarray(np.asarray(inp["hgrn_lb"]).reshape(2, 8, 128).transpose(2, 0, 1))
    w["c_g"] = np.ascontiguousarray(np.asarray(inp["hgrn_norm_g"])[0].reshape(8, 128).T)
    lng = np.asarray(inp["ln_g"]).reshape(4, 1, 1024)
    lnb = np.asarray(inp["ln_b"]).reshape(4, 1, 1024)
    w["lng"] = np.ascontiguousarray(np.broadcast_to(lng, (4, 128, 1024)))
    w["lnb"] = np.ascontiguousarray(np.broadcast_to(lnb, (4, 128, 1024)))
    for l in range(2):
        w[f"p{l}_wq"] = np.ascontiguousarray(np.asarray(inp["peer_w_q"])[l])
        sk = np.asarray(inp["peer_sub_keys"])[l]
        w[f"p{l}_keysT"] = np.ascontiguousarray(sk.transpose(3, 0, 1, 2).reshape(128, 16, 128))
        u = np.asarray(inp["peer_u"])[l]
        w[f"p{l}_ut"] = np.ascontiguousarray(u.reshape(128, 128, 8, 128).transpose(1, 3, 2, 0))
        v = np.asarray(inp["peer_v"])[l]
        w[f"p{l}_v"] = np.ascontiguousarray(v.reshape(128, 128, 1024).transpose(1, 0, 2))
    return w


_DT = {np.dtype(np.float32): F32, np.dtype(ml_dtypes.bfloat16): BF16}


def build(S, stages, shapes):
    nc = bass.Bass("TRN2", target_bir_lowering=False)
    dram = {}
    for name, (shp, dt) in shapes.items():
        dram[name] = nc.dram_tensor(name, list(shp), _DT[np.dtype(dt)], kind="ExternalInput").ap()
    x = nc.dram_tensor("x", [S, D], F32, kind="ExternalInput").ap()
    out = nc.dram_tensor("out", [S, D], F32, kind="ExternalOutput").ap()
    p = Prog(nc)
    for sname in stages:
        if sname.startswith("peer"):
            l = sname[-1]
            dram[f"p{l}_ut16"] = nc.dram_tensor(f"p{l}_ut16", [128, 128, 8, 128], BF16, kind="Internal").ap()
            dram[f"p{l}_v16"] = nc.dram_tensor(f"p{l}_v16", [128, 128, 1024], BF16, kind="Internal").ap()
    cur = x
    for si, sname in enumerate(stages):
        if si == len(stages) - 1:
            nxt = out
        else:
            nxt = nc.dram_tensor(f"hbuf{si}", [S, D], F32, kind="Internal").ap()
        if sname == "ab":
            stage_ab(p, dram, cur, nxt, S)
        elif sname == "c":
            stage_c(p, dram, cur, nxt, S)
        elif sname in ("peer0", "peer1"):
            stage_peer(p, dram, cur, nxt, S, int(sname[-1]))
        cur = nxt
    p.finalize()
    return nc


def stage_inputs(stages):
    need = {"ident", "lng", "lnb"}
    for s in stages:
        if s == "ab":
            need |= {"ab_wfm", "ab_wtm", "ab_wout", "ab_poolw", "ab_pscale", "ab_retg", "ab_cos", "ab_sin",
                     "ab_dq", "ab_dk", "ab_maskT", "ab_pm"}
        elif s == "c":
            need |= {"c_wfm", "c_wtm", "c_wout", "c_lb", "c_g", "c_maskC", "c_m96"}
        else:
            l = s[-1]
            need |= {f"p{l}_wq", f"p{l}_keysT", f"p{l}_ut", f"p{l}_v", "p_bmask", "p_iota"}
    return need


STAGES = ["ab", "peer0", "c", "peer1"]


def kernel(**inputs):
    S = 4096
    shared = dict(_consts(S))
    shared.update(_prep_weights(inputs))
    shapes = {k: (v.shape, v.dtype) for k, v in shared.items()}
    nc = build(S, STAGES, shapes)
    x = np.asarray(inputs["x"])
    in_maps = [dict(shared, x=np.ascontiguousarray(x[b])) for b in range(8)]
    res = run_bass_kernel_spmd(nc, in_maps, core_ids=list(range(8)))
    return np.stack([np.asarray(r["out"]) for r in res.results], 0).astype(np.float32)
```
